# Optimizing a Trainium2 kernel written in Bass

```python
import math
import jax, jax.numpy as jnp
from jax import lax
import numpy as np

D_MODEL = 2048
BATCH = 4
SEQ = 2048
DEPTH = 1

HGRN_WIDTH = D_MODEL // 2
HGRN_HEAD_DIM = 128
HGRN_HEADS = HGRN_WIDTH // HGRN_HEAD_DIM
CHUNK = 64
ATTN_WIDTH = D_MODEL - HGRN_WIDTH
ATTN_HEAD_DIM = 128
ATTN_HEADS = ATTN_WIDTH // ATTN_HEAD_DIM
ATTN_KV_HEADS = 2
KV_WIDTH = ATTN_KV_HEADS * ATTN_HEAD_DIM
WINDOW = 128
ATTN_BLOCK = 128
KEY_SPAN = ATTN_BLOCK + 2 * WINDOW
REL_BUCKETS = 32
REL_MAX_DIST = 128
D_FF = 5632
EPS = 1e-6
NEG_INF = -1e30
IN_SPLITS = (HGRN_WIDTH, HGRN_WIDTH, HGRN_WIDTH, HGRN_WIDTH, HGRN_WIDTH, ATTN_WIDTH, KV_WIDTH, KV_WIDTH)
IN_COLS = sum(IN_SPLITS)

kernel_name = "hymba_hgrn2_swa_macaron_sandwich"


def rms_norm(x, gain):
    xf = x.astype(jnp.float32)
    y = xf * lax.rsqrt(jnp.mean(xf * xf, axis=-1, keepdims=True) + EPS)
    return (y * gain.astype(jnp.float32)).astype(x.dtype)


def swiglu(x, w_gate_up, w_down):
    gate, up = jnp.split(x @ w_gate_up, 2, axis=-1)
    return (jax.nn.silu(gate) * up) @ w_down


def hgrn_chunk_scan(q, k, v, log_f):
    b_, h_, l_, dk = q.shape
    dv = v.shape[-1]
    n = l_ // CHUNK
    q = q.reshape(b_, h_, n, CHUNK, dk)
    k = k.reshape(b_, h_, n, CHUNK, dk)
    log_f = log_f.reshape(b_, h_, n, CHUNK, dk)
    v = v.reshape(b_, h_, n, CHUNK, dv)
    cum = jnp.cumsum(log_f, axis=3)
    last = cum[:, :, :, -1:, :]
    q_dec = q * jnp.exp(cum)
    k_dec = k * jnp.exp(-cum)
    k_tail = k * jnp.exp(last - cum)
    lower = jnp.tril(jnp.ones((CHUNK, CHUNK), dtype=bool))
    scores = jnp.einsum('bhnck,bhnsk->bhncs', q_dec, k_dec)
    scores = jnp.where(lower, scores, 0.0)
    o_intra = jnp.einsum('bhncs,bhnsv->bhncv', scores, v)
    kv_chunk = jnp.einsum('bhnsk,bhnsv->bhnkv', k_tail, v)
    chunk_decay = jnp.exp(last[:, :, :, 0, :])

    def step(state, inp):
        kv_n, dec_n = inp
        return dec_n[..., None] * state + kv_n, state

    init = jnp.zeros((b_, h_, dk, dv), kv_chunk.dtype)
    _, prev = lax.scan(step, init, (jnp.moveaxis(kv_chunk, 2, 0), jnp.moveaxis(chunk_decay, 2, 0)))
    prev = jnp.moveaxis(prev, 0, 2)
    o_inter = jnp.einsum('bhnck,bhnkv->bhncv', q_dec, prev)
    return (o_intra + o_inter).reshape(b_, h_, l_, dv)


def hgrn2_mixer(q, i, f_fwd_logit, f_bwd_logit, g, lb_fwd, lb_bwd, out_gain):
    b_, l_, _ = q.shape

    def heads(t):
        return t.astype(jnp.float32).reshape(b_, l_, HGRN_HEADS, HGRN_HEAD_DIM).transpose(0, 2, 1, 3)

    qh, vh = heads(q), heads(i)

    def direction(f_logit, lb, flip):
        lb = lb.astype(jnp.float32).reshape(HGRN_HEADS, 1, HGRN_HEAD_DIM)
        f = lb + (1.0 - lb) * jax.nn.sigmoid(heads(f_logit))
        log_f, k = jnp.log(f), 1.0 - f
        qq, vv = qh, vh
        if flip:
            qq, vv, k, log_f = (jnp.flip(t, axis=2) for t in (qq, vv, k, log_f))
        o = hgrn_chunk_scan(qq, k, vv, log_f)
        return jnp.flip(o, axis=2) if flip else o

    o = direction(f_fwd_logit, lb_fwd, False) + direction(f_bwd_logit, lb_bwd, True)
    o = o.transpose(0, 2, 1, 3)
    o = o * lax.rsqrt(jnp.mean(o * o, axis=-1, keepdims=True) + EPS)
    o = o * out_gain.astype(jnp.float32).reshape(HGRN_HEADS, HGRN_HEAD_DIM)
    o = o.reshape(b_, l_, HGRN_WIDTH) * jax.nn.silu(g.astype(jnp.float32))
    return o.astype(q.dtype)


def t5_buckets(rel):
    nb = REL_BUCKETS // 2
    max_exact = nb // 2
    bucket = (rel > 0).astype(np.int32) * nb
    n = np.abs(rel)
    large = max_exact + (np.log(np.maximum(n, 1) / max_exact) / np.log(REL_MAX_DIST / max_exact)
                         * (nb - max_exact)).astype(np.int32)
    large = np.minimum(large, nb - 1)
    return bucket + np.where(n < max_exact, n, large).astype(np.int32)


def window_attention(q, k, v, sink, rel_table):
    b_, l_, _ = q.shape
    nb = l_ // ATTN_BLOCK
    grp = ATTN_HEADS // ATTN_KV_HEADS
    qb = q.reshape(b_, nb, ATTN_BLOCK, ATTN_KV_HEADS, grp, ATTN_HEAD_DIM)

    def band(t):
        tp = jnp.pad(t, ((0, 0), (WINDOW, WINDOW), (0, 0)))
        tp = tp.reshape(b_, nb + 2, ATTN_BLOCK, ATTN_KV_HEADS, ATTN_HEAD_DIM)
        return jnp.concatenate([tp[:, :-2], tp[:, 1:-1], tp[:, 2:]], axis=2)

    kb, vb = band(k), band(v)
    scores = jnp.einsum('bncxgd,bnsxd->bxgncs', qb, kb).astype(jnp.float32) / math.sqrt(ATTN_HEAD_DIM)
    c = np.arange(ATTN_BLOCK)[:, None]
    s = np.arange(KEY_SPAN)[None, :]
    rel = s - WINDOW - c
    bias = rel_table.astype(jnp.float32)[t5_buckets(rel)]
    bias = jnp.transpose(bias, (2, 0, 1)).reshape(ATTN_KV_HEADS, grp, 1, ATTN_BLOCK, KEY_SPAN)
    key_pos = np.arange(nb)[:, None, None] * ATTN_BLOCK - WINDOW + s[None]
    valid = (np.abs(rel)[None] <= WINDOW) & (key_pos >= 0) & (key_pos < l_)
    scores = jnp.where(valid, scores + bias, NEG_INF)
    sink_col = jnp.broadcast_to(sink.astype(jnp.float32).reshape(1, ATTN_KV_HEADS, grp, 1, 1, 1),
                                scores.shape[:-1] + (1,))
    probs = jax.nn.softmax(jnp.concatenate([scores, sink_col], axis=-1), axis=-1)[..., :KEY_SPAN]
    out = jnp.einsum('bxgncs,bnsxd->bncxgd', probs.astype(v.dtype), vb)
    return out.reshape(b_, l_, ATTN_WIDTH)


def setup_inputs(seed: int = 0) -> dict:
    key = jax.random.key(seed)
    ks = jax.random.split(key, 20)
    f32 = jnp.float32

    def w(k, shape, fan_in):
        return jax.random.normal(k, shape, f32) * fan_in ** -0.5

    def gain(k, shape):
        return 1.0 + 0.02 * jax.random.normal(k, shape, f32)

    return {
        "x": jax.random.normal(ks[0], (BATCH, SEQ, D_MODEL), f32),
        "pre_norm_ffn1": gain(ks[1], (DEPTH, D_MODEL)),
        "post_norm_ffn1": gain(ks[2], (DEPTH, D_MODEL)),
        "w_ffn1_gate_up": w(ks[3], (DEPTH, D_MODEL, 2 * D_FF), D_MODEL),
        "w_ffn1_down": w(ks[4], (DEPTH, D_FF, D_MODEL), D_FF),
        "pre_norm_mix": gain(ks[5], (DEPTH, D_MODEL)),
        "post_norm_mix": gain(ks[6], (DEPTH, D_MODEL)),
        "w_mix_in": w(ks[7], (DEPTH, D_MODEL, IN_COLS), D_MODEL),
        "hgrn_lower_bounds_fwd": 0.1 * jax.random.normal(ks[8], (DEPTH + 1, HGRN_WIDTH), f32),
        "hgrn_lower_bounds_bwd": 0.1 * jax.random.normal(ks[9], (DEPTH + 1, HGRN_WIDTH), f32),
        "hgrn_out_norm": gain(ks[10], (DEPTH, HGRN_WIDTH)),
        "attn_sink": 0.5 * jax.random.normal(ks[11], (DEPTH, ATTN_HEADS), f32),
        "w_mix_out": w(ks[12], (DEPTH, HGRN_WIDTH + ATTN_WIDTH, D_MODEL), HGRN_WIDTH + ATTN_WIDTH),
        "pre_norm_ffn2": gain(ks[13], (DEPTH, D_MODEL)),
        "post_norm_ffn2": gain(ks[14], (DEPTH, D_MODEL)),
        "w_ffn2_gate_up": w(ks[15], (DEPTH, D_MODEL, 2 * D_FF), D_MODEL),
        "w_ffn2_down": w(ks[16], (DEPTH, D_FF, D_MODEL), D_FF),
        "rel_bias_table": 0.5 * jax.random.normal(ks[17], (REL_BUCKETS, ATTN_HEADS), f32),
    }


def reference(x, pre_norm_ffn1, post_norm_ffn1, w_ffn1_gate_up, w_ffn1_down, pre_norm_mix,
              post_norm_mix, w_mix_in, hgrn_lower_bounds_fwd, hgrn_lower_bounds_bwd, hgrn_out_norm,
              attn_sink, w_mix_out, pre_norm_ffn2, post_norm_ffn2, w_ffn2_gate_up, w_ffn2_down,
              rel_bias_table):
    lb_fwd_all = jnp.cumsum(jax.nn.softmax(hgrn_lower_bounds_fwd.astype(jnp.float32), axis=0), axis=0)
    lb_bwd_all = jnp.cumsum(jax.nn.softmax(hgrn_lower_bounds_bwd.astype(jnp.float32), axis=0), axis=0)
    split_at = np.cumsum(IN_SPLITS)[:-1].tolist()
    for layer in range(DEPTH):
        ff = swiglu(rms_norm(x, pre_norm_ffn1[layer]), w_ffn1_gate_up[layer], w_ffn1_down[layer])
        x = x + 0.5 * rms_norm(ff, post_norm_ffn1[layer])
        h = rms_norm(x, pre_norm_mix[layer])
        q_h, i_h, f_fwd, f_bwd, g_h, q_a, k_a, v_a = jnp.split(h @ w_mix_in[layer], split_at, axis=-1)
        y_h = hgrn2_mixer(q_h, i_h, f_fwd, f_bwd, g_h, lb_fwd_all[layer], lb_bwd_all[layer],
                          hgrn_out_norm[layer])
        y_a = window_attention(q_a, k_a, v_a, attn_sink[layer], rel_bias_table)
        mixed = jnp.concatenate([y_h.astype(x.dtype), y_a.astype(x.dtype)], axis=-1) @ w_mix_out[layer]
        x = x + rms_norm(mixed, post_norm_mix[layer])
        ff = swiglu(rms_norm(x, pre_norm_ffn2[layer]), w_ffn2_gate_up[layer], w_ffn2_down[layer])
        x = x + 0.5 * rms_norm(ff, post_norm_ffn2[layer])
    return x
```

```python
import contextlib
import math
import numpy as np
import concourse.bass as bass
import concourse.mybir as mybir
from concourse.bass_utils import run_bass_kernel_spmd

F32 = mybir.dt.float32
BF16 = mybir.dt.bfloat16
AF = mybir.ActivationFunctionType
ALU = mybir.AluOpType
AX = mybir.AxisListType

D = 2048
T = 1024
DFF = 5632
NJ = 44
NG = 4
JG = 11
EPS = 1e-6
ENGS = ("pe", "act", "dve", "pool", "sp")
EPOCH = 20000
NEG = -10000.0
DBG = {}


class Op:
    __slots__ = ("eng", "fn", "deps", "is_dma", "dsem", "dval", "dinc", "idx", "signal", "tick")


class Sched:
    def __init__(self, nc):
        self.nc = nc
        self.ops = {e: [] for e in ENGS}
        self.last_writer = {}
        self.readers = {}
        self.dma_sem_count = {}
        self.all_ops = []
        self.barrier_deps = []
        self.since_barrier_dma = []

    def add(self, eng, fn, reads=(), writes=(), dma_sem=None, dinc=16):
        op = Op()
        op.eng = eng
        op.fn = fn
        op.is_dma = dma_sem is not None
        op.dsem = dma_sem
        op.dinc = dinc
        op.signal = False
        op.tick = None
        deps = list(self.barrier_deps)
        for k in reads:
            w = self.last_writer.get(k)
            if w is not None:
                deps.append(w)
        for k in writes:
            w = self.last_writer.get(k)
            if w is not None:
                deps.append(w)
            deps.extend(self.readers.get(k, ()))
        fd = []
        seen = set()
        for d in deps:
            if id(d) in seen:
                continue
            seen.add(id(d))
            fd.append(d)
        op.deps = fd
        if op.is_dma:
            c = self.dma_sem_count.get(dma_sem, 0) + dinc
            self.dma_sem_count[dma_sem] = c
            op.dval = c
            self.since_barrier_dma.append(op)
        for k in reads:
            self.readers.setdefault(k, []).append(op)
        for k in writes:
            self.last_writer[k] = op
            self.readers[k] = []
        self.ops[eng].append(op)
        self.all_ops.append(op)
        return op

    def barrier(self):
        deps = []
        for e in ENGS:
            for op in reversed(self.ops[e]):
                if not op.is_dma:
                    deps.append(op)
                    break
        deps.extend(self.since_barrier_dma)
        self.since_barrier_dma = []
        self.barrier_deps = deps

    def emit(self, final_waits=()):
        nc = self.nc
        for op in self.all_ops:
            nd = []
            for d in op.deps:
                if d.is_dma:
                    nd.append(d)
                    continue
                if d.eng == op.eng and (not op.is_dma) and op.eng == "pe":
                    continue
                nd.append(d)
            op.deps = nd
            for d in nd:
                if not d.is_dma:
                    d.signal = True
        for op in final_waits:
            if not op.is_dma:
                op.signal = True
        nticks = {}
        for e in ENGS:
            t = 0
            for op in self.ops[e]:
                if op.signal and not op.is_dma:
                    t += 1
                    op.tick = t
            nticks[e] = t
        stack = contextlib.ExitStack()
        esems = {}
        for e in ENGS:
            n = max(1, (nticks[e] + EPOCH - 1) // EPOCH)
            esems[e] = [stack.enter_context(nc.semaphore(f"c_{e}_{i}")) for i in range(n)]
        dsems = {k: stack.enter_context(nc.semaphore(f"d_{k}")) for k in self.dma_sem_count}

        def sig_of(d):
            if d.is_dma:
                return (dsems[d.dsem], d.dval, ("d", d.dsem))
            ep = (d.tick - 1) // EPOCH
            return (esems[d.eng][ep], (d.tick - 1) % EPOCH + 1, ("c", d.eng, ep))

        engobj = {"pe": "tensor", "act": "scalar", "dve": "vector", "pool": "gpsimd", "sp": "sync"}
        with stack:
            with nc.Block() as block:
                for e in ENGS:
                    ops = self.ops[e]
                    extra = list(final_waits) if e == "sp" else []
                    if not ops and not extra:
                        continue

                    def body(eng, ops=ops, e=e, extra=extra):
                        known = {}
                        for op in ops:
                            need = {}
                            for d in op.deps:
                                s, v, key = sig_of(d)
                                if known.get(key, 0) >= v:
                                    continue
                                if key not in need or need[key][1] < v:
                                    need[key] = (s, v)
                            for key, (s, v) in need.items():
                                eng.wait_ge(s, v)
                                known[key] = v
                            ins = op.fn(eng)
                            if op.is_dma:
                                ins.then_inc(dsems[op.dsem], op.dinc)
                            elif op.signal:
                                ep = (op.tick - 1) // EPOCH
                                ins.then_inc(esems[e][ep], 1)
                        for w in extra:
                            s, v, key = sig_of(w)
                            if known.get(key, 0) < v:
                                eng.wait_ge(s, v)
                                known[key] = v

                    getattr(block, engobj[e])(body)


V_PRE1, V_POST1, V_PREM, V_POSTM, V_PRE2, V_POST2 = 0, 16, 32, 48, 64, 80
V_OG, V_SINK, V_M0, V_M1 = 96, 104, 112, 113
NVEC = 160
V_A0, V_A1 = 128, 144

ARENA_BYTES = 211968
OFF_X, OFF_H, OFF_W, OFF_B, OFF_M = 0, 65536, 98304, 114688, 202752


def build(stage=99, debug=False):
    nc = bass.Bass("TRN2", target_bir_lowering=False)

    def din(name, shape, dt=F32):
        return nc.dram_tensor(name, list(shape), dt, kind="ExternalInput")

    xT_d = din("xT", [128, 16, T])
    wgu_d = [din("wgu1", [NJ, 128, 2 * 16 * 128]), din("wgu2", [NJ, 128, 2 * 16 * 128])]
    wd_d = [din("wd1", [NG, 8, 128, 2 * JG * 128]), din("wd2", [NG, 8, 128, 2 * JG * 128])]
    win_s_d = din("win_s", [8, 128, 2 * 16 * 128])
    win_a_d = din("win_a", [8, 128, 2 * 16 * 128])
    win_b_d = din("win_b", [8, 128, 2 * 16 * 128])
    win_g_d = din("win_g", [8, 128, 16 * 128])
    win_kv_d = din("win_kv", [2, 128, 2 * 16 * 128])
    win_q_d = din("win_q", [4, 128, 2 * 16 * 128])
    wout_d = din("wout", [8, 128, 2 * 16 * 128])
    vecs_d = din("vecs", [128, NVEC])
    araw_d = din("araw", [8, 128, 2 * 256])
    bias_d = din("bias", [8, 128, 768])
    cm_d = din("cm", [128, 4 * 128 + 128 + 2])
    ident_d = din("ident", [128, 128])
    maskm_d = din("maskm", [128, 1025])
    out_d = nc.dram_tensor("outT", [128, 16, T], F32, kind="ExternalOutput")
    dbg_d = nc.dram_tensor("dbg", [128, 16, T], F32, kind="ExternalOutput") if debug else None
    halo_in = nc.dram_tensor("halo_in", [128, 512], BF16)
    halo_out = nc.dram_tensor("halo_out", [256, 512], BF16)
    st_in = nc.dram_tensor("st_in", [128, 1024], F32)
    st_out = nc.dram_tensor("st_out", [256, 1024], F32)
    RG = [[0, 1], [2, 3], [4, 5], [6, 7]]

    S = Sched(nc)
    st = contextlib.ExitStack()
    with st:
        arena = st.enter_context(nc.sbuf_tensor("arena", [128, ARENA_BYTES // 4], F32))
        arena_bf = arena.bitcast(BF16)
        psb = [st.enter_context(nc.psum_tensor(f"ps{i}", [128, 512], F32)) for i in range(8)]

        def cf(off, n):
            assert off % 4 == 0
            return arena[:, off // 4: off // 4 + n]

        def cb(off, n):
            assert off % 2 == 0
            return arena_bf[:, off // 2: off // 2 + n]

        xT = cf(OFF_X, 16 * T).rearrange("p (c t) -> p c t", c=16)
        hT = cb(OFF_H, 16 * T).rearrange("p (c t) -> p c t", c=16)
        wslot = [cb(OFF_W, 4096), cb(OFF_W + 8192, 4096)]
        vecs = cf(OFF_M, NVEC)
        o_ = OFF_M + NVEC * 4
        cm = cf(o_, 4 * 128 + 128 + 2)
        o_ += (4 * 128 + 128 + 2) * 4
        o_ = (o_ + 31) // 32 * 32
        ident = cb(o_, 128)
        o_ += 256
        ftmp = [cf(o_, 512), cf(o_ + 2048, 512)]
        o_ += 4096
        ones_b = cb(o_, 128)
        o_ += 256
        mk_b = cb(o_, 512)
        o_ += 1024
        indb = cb(o_, 2)
        o_ += 32
        assert o_ <= ARENA_BYTES, o_
        M_F, M_B, Mx_F, Mx_B = (cm[:, i * 128:(i + 1) * 128] for i in range(4))
        ones_f = cm[:, 512:640]
        ind = cm[:, 640:642]

        ffT = cf(OFF_B, 16 * T).rearrange("p (c t) -> p c t", c=16)
        aT = cb(OFF_B + 65536, JG * T).rearrange("p (j t) -> p j t", j=JG)
        nsq = [cb(OFF_B + 65536, 512), cb(OFF_B + 65536 + 2048, 512)]
        nrs = cf(OFF_B + 65536 + 4096, 512)
        nrstd = cf(OFF_B + 65536 + 6144, 512)
        ntmp = [cf(OFF_B + 65536 + 8192, 512), cf(OFF_B + 65536 + 10240, 512)]

        state = {"pb": 0, "ws": 0, "u": 0}

        def nb():
            b = state["pb"]
            state["pb"] = (b + 1) % 8
            return b

        def uid():
            state["u"] += 1
            return state["u"]

        def wload(src_ap, nelem):
            s = state["ws"]
            state["ws"] = 1 - s
            dst = wslot[s][:, 0:nelem]
            S.add("pool", lambda e: e.dma_start(out=dst, in_=src_ap), writes=[("w", s)], dma_sem=f"w{s}")
            return s

        def MM(out, lhsT, rhs, start, stop, reads, writes):
            return S.add("pe", lambda e: e.matmul(out, lhsT=lhsT, rhs=rhs, start=start, stop=stop),
                         reads=reads, writes=writes)

        def ACT(out, in_, func, reads, writes, scale=1.0, bias=0.0, accum_out=None):
            def f(e):
                kw = {}
                if accum_out is not None:
                    kw["accum_out"] = accum_out
                return e.activation(out=out, in_=in_, func=func, scale=scale, bias=bias, **kw)
            return S.add("act", f, reads=reads, writes=writes)

        def TS(out, in0, s1, op0, reads, writes, s2=None, op1=None, eng="dve"):
            def f(e):
                if op1 is None:
                    return e.tensor_scalar(out=out, in0=in0, scalar1=s1, scalar2=None, op0=op0)
                return e.tensor_scalar(out=out, in0=in0, scalar1=s1, scalar2=s2, op0=op0, op1=op1)
            return S.add(eng, f, reads=reads, writes=writes)

        def TT(out, in0, in1, op, reads, writes, eng="dve"):
            return S.add(eng, lambda e: e.tensor_tensor(out=out, in0=in0, in1=in1, op=op), reads=reads, writes=writes)

        def STT(out, in0, scalar, in1, op0, op1, reads, writes):
            return S.add("dve", lambda e: e.scalar_tensor_tensor(out=out, in0=in0, scalar=scalar, in1=in1, op0=op0, op1=op1),
                         reads=reads, writes=writes)

        def RECIP(out, in_, reads, writes):
            return S.add("dve", lambda e: e.reciprocal(out=out, in_=in_), reads=reads, writes=writes)

        def COPY(out, in_, reads, writes, eng="dve"):
            if eng == "act":
                return S.add("act", lambda e: e.activation(out=out, in_=in_, func=AF.Copy), reads=reads, writes=writes)
            return S.add(eng, lambda e: e.tensor_copy(out=out, in_=in_), reads=reads, writes=writes)

        def DMA(out, in_, reads, writes, sem, eng="sp"):
            return S.add(eng, lambda e: e.dma_start(out=out, in_=in_), reads=reads, writes=writes, dma_sem=sem)

        DMA(vecs, vecs_d.ap(), [], ["vecs"], "c0")
        DMA(cm, cm_d.ap(), [], ["cm"], "c1")
        DMA(ident, ident_d.ap(), [], ["ident"], "c2", eng="pool")
        DMA(ones_b, cm_d.ap()[:, 512:640], [], ["cmb"], "c3", eng="pool")
        DMA(mk_b, cm_d.ap()[:, 0:512], [], ["cmb"], "c4", eng="pool")
        DMA(indb, cm_d.ap()[:, 640:642], [], ["cmb"], "c5", eng="pool")
        for q in range(4):
            DMA(xT[:, 4 * q:4 * q + 4, :], xT_d.ap()[:, 4 * q:4 * q + 4, :], [], [("x", c) for c in range(4 * q, 4 * q + 4)], f"x{q}")

        def gcol(base, c):
            return vecs[:, base + c: base + c + 1]

        def rstd_of(src, srckey, nfeat_chunks, half, scratch_sq, rs, rstd, denom):
            b = nb()
            for c in range(nfeat_chunks):
                sq = scratch_sq[c % 2]
                ACT(sq, src(c, half), AF.Square, [srckey(c)], [("nsq", c % 2)])
                MM(psb[b][:, :], ones_b, sq, c == 0, c == nfeat_chunks - 1, [("nsq", c % 2), "cmb"], [("ps", b)])
            ACT(rs, psb[b][:, :], AF.Sqrt, [("ps", b)], ["nrs"], scale=1.0 / denom, bias=EPS)
            RECIP(rstd, rs, ["nrs"], ["nrstd"])

        def prenorm(gbase):
            for half in range(2):
                hs = slice(half * 512, (half + 1) * 512)
                rstd_of(lambda c, h: xT[:, c, h * 512:(h + 1) * 512], lambda c: ("x", c), 16, half, nsq, nrs, nrstd, float(D))
                for c in range(16):
                    STT(hT[:, c, hs], xT[:, c, hs], gcol(gbase, c), nrstd, ALU.mult, ALU.mult,
                        [("x", c), "nrstd", "vecs"], [("h", c, half)])

        def postnorm_residual(gbase, coef, src=None):
            if src is None:
                src = lambda c, h: ffT[:, c, h * 512:(h + 1) * 512]
            for half in range(2):
                hs = slice(half * 512, (half + 1) * 512)
                rstd_of(src, lambda c: ("ff", c), 16, half, nsq, nrs, nrstd, float(D))
                for c in range(16):
                    tmp = ntmp[c % 2]
                    STT(tmp, src(c, half), gcol(gbase, c), nrstd, ALU.mult, ALU.mult,
                        [("ff", c), "nrstd", "vecs"], [("ntmp", c % 2)])
                    STT(xT[:, c, hs], tmp, coef, xT[:, c, hs], ALU.mult, ALU.add,
                        [("ntmp", c % 2), ("x", c)], [("x", c)])

        def ffn(wgu, wd, gpre, gpost):
            prenorm(gpre)
            for g in range(NG):
                for jj in range(JG):
                    j = g * JG + jj
                    s = wload(wgu.ap()[j], 4096)
                    w = wslot[s].rearrange("p (a k n) -> p a k n", a=2, k=16)
                    banks = [nb() for _ in range(4)]
                    for a in range(2):
                        for kc in range(16):
                            for half in range(2):
                                b = banks[a * 2 + half]
                                MM(psb[b][:, :], w[:, a, kc, :], hT[:, kc, half * 512:(half + 1) * 512], kc == 0, kc == 15,
                                   [("w", s), ("h", kc, half)], [("ps", b)])
                    for half in range(2):
                        bg, bu = banks[half], banks[2 + half]
                        t = ftmp[half]
                        ACT(t, psb[bg][:, :], AF.Silu, [("ps", bg)], [("ftmp", half)])
                        TT(aT[:, jj, half * 512:(half + 1) * 512], psb[bu][:, :], t, ALU.mult,
                           [("ps", bu), ("ftmp", half)], [("a", jj, half)])
                for i2 in range(8):
                    s = wload(wd.ap()[g, i2], 2 * JG * 128)
                    w = wslot[s][:, 0:2 * JG * 128].rearrange("p (i j n) -> p i j n", i=2, j=JG)
                    for ii in range(2):
                        i = 2 * i2 + ii
                        for half in range(2):
                            b = nb()
                            for jj in range(JG):
                                MM(psb[b][:, :], w[:, ii, jj, :], aT[:, jj, half * 512:(half + 1) * 512], jj == 0, jj == JG - 1,
                                   [("w", s), ("a", jj, half)], [("ps", b)])
                            dst = ffT[:, i, half * 512:(half + 1) * 512]
                            if g == 0:
                                COPY(dst, psb[b][:, :], [("ps", b)], [("ff", i)])
                            else:
                                TT(dst, psb[b][:, :], dst, ALU.add, [("ps", b), ("ff", i)], [("ff", i)])
            S.barrier()
            postnorm_residual(gpost, 0.5)


        yT = cb(OFF_B, 16 * T).rearrange("p (c t) -> p c t", c=16)
        kTe = cb(OFF_B + 32768, 2 * 1152).rearrange("p (u t) -> p u t", u=2)
        vE = cb(OFF_B + 37376, 2 * 9 * 128).rearrange("p (u t d) -> p u t d", u=2, t=9)
        SCR = OFF_B + 41984
        ISQ = 1.0 / math.sqrt(128.0)

        def featmaj(w_unit, dst_fn, evac="dve"):
            for half in range(2):
                b = nb()
                for kc in range(16):
                    MM(psb[b][:, :], w_unit[:, kc, :], hT[:, kc, half * 512:(half + 1) * 512], kc == 0, kc == 15,
                       [("w", 0), ("w", 1), ("h", kc, half)], [("ps", b)])
                dst_fn(half, b)

        def tokmaj(w_pair, ncols, fn):
            for tt in range(8):
                b = nb()
                for kc in range(16):
                    MM(psb[b][:, 0:ncols].rearrange("p (u n) -> p u n", n=128), hT[:, kc, tt * 128:(tt + 1) * 128], w_pair[:, :, kc, :], kc == 0, kc == 15,
                       [("w", 0), ("w", 1), ("h", kc, tt // 4)], [("ps", b)])
                fn(tt, b)

        def mixer_kv():
            s = wload(win_kv_d.ap()[0], 4096)
            w = wslot[s].rearrange("p (a k n) -> p a k n", a=2, k=16)
            for u in range(2):
                featmaj(w[:, u], lambda half, b, u=u: COPY(kTe[:, u, half * 512:(half + 1) * 512], psb[b][:, :], [("ps", b)], [("kT", u)]))
            s = wload(win_kv_d.ap()[1], 4096)
            w2 = wslot[s].rearrange("p (a k n) -> p a k n", a=2, k=16)
            tokmaj(w2, 256, lambda tt, b: COPY(vE[:, :, tt, :], psb[b][:, 0:256].rearrange("p (u d) -> p u d", u=2), [("ps", b)], [("vE", tt)]))
            hp = cb(SCR, 1024).rearrange("p (r n) -> p r n", r=2)
            d1 = [DMA(halo_in.ap()[:, u * 128:(u + 1) * 128], kTe[:, u, 896:1024], [("kT", u)], ["halo_in"], f"hi{u}") for u in range(2)]
            d1 += [DMA(halo_in.ap()[:, 256 + u * 128:256 + (u + 1) * 128], vE[:, u, 7, :], [("vE", 7)], ["halo_in"], f"hi{2 + u}") for u in range(2)]
            S.add("pool", lambda e: e.collective_compute("AllGather", ALU.bypass, replica_groups=RG, ins=[halo_in.ap().opt()], outs=[halo_out.ap().opt()]),
                  reads=["halo_in"], writes=["halo_out"], dma_sem="cc1", dinc=1)
            DMA(hp, halo_out.ap().rearrange("(r p) n -> p r n", p=128), ["halo_out"], ["hp"], "hp")
            tmpb = cf(SCR + 2048, 512)
            TS(tmpb, hp[:, 0, :], gcol(V_M0, 0), ALU.mult, ["hp", "vecs"], ["hpt"])
            res = cb(SCR + 4096, 512)
            STT(res, hp[:, 1, :], gcol(V_M1, 0), tmpb, ALU.mult, ALU.add, ["hp", "hpt", "vecs"], ["hres"])
            for u in range(2):
                COPY(kTe[:, u, 1024:1152], res[:, u * 128:(u + 1) * 128], ["hres"], [("kT", u)])
                COPY(vE[:, u, 8, :], res[:, 256 + u * 128:256 + (u + 1) * 128], ["hres"], [("vE", 8)])

        def attention():
            NP = 4
            qsb = cb(SCR, 1024)
            bias_sb = cf(SCR + 2048, 768)
            s_sb = [cf(SCR + 5120 + i * 1536, 384) for i in range(NP)]
            p_f = [cf(SCR + 11264 + i * 1536, 384) for i in range(NP)]
            p_b = [cb(SCR + 17408 + i * 768, 384) for i in range(NP)]
            pT_sb = [cb(SCR + 20480 + i * 768, 384) for i in range(NP)]
            cols = cf(SCR + 23552, 8 * NP)
            for a in range(4):
                s = wload(win_q_d.ap()[a], 4096)
                w = wslot[s].rearrange("p (a k n) -> p a k n", a=2, k=16)
                for u in range(2):
                    head = 2 * a + u
                    kvh = head // 4
                    featmaj(w[:, u], lambda half, b: COPY(qsb[:, half * 512:(half + 1) * 512], psb[b][:, :], [("ps", b)], ["qsb"]))
                    DMA(bias_sb, bias_d.ap()[head], [], ["bias"], "bias")
                    sink = gcol(V_SINK, head)
                    for grp in range(2):
                        ctx = []
                        for par in range(NP):
                            blk = grp * NP + par
                            lo = max(0, blk - 1) * 128
                            hi = (blk + 2) * 128
                            ctx.append(dict(blk=blk, par=par, lo=lo, nk=hi - lo, c0=cols[:, par * 8:par * 8 + 8],
                                            bsl=(bias_sb[:, 128:384] if blk == 0 else (bias_sb[:, 384:768] if blk == 7 else bias_sb[:, 0:384]))))

                        def st1(c):
                            c["b"] = nb()
                            MM(psb[c["b"]][:, 0:c["nk"]], qsb[:, c["blk"] * 128:(c["blk"] + 1) * 128], kTe[:, kvh, c["lo"]:c["lo"] + c["nk"]], True, True,
                               ["qsb", ("kT", kvh)], [("ps", c["b"])])

                        def st2(c):
                            STT(s_sb[c["par"]][:, 0:c["nk"]], psb[c["b"]][:, 0:c["nk"]], ISQ, c["bsl"], ALU.mult, ALU.add, [("ps", c["b"]), "bias"], [("s", c["par"])])

                        def st3(c):
                            par, nk, c0 = c["par"], c["nk"], c["c0"]
                            S.add("dve", lambda e: e.reduce_max(out=c0[:, 0:1], in_=s_sb[par][:, 0:nk], axis=AX.X),
                                  reads=[("s", par)], writes=[("cm0", par)])

                        def st4(c):
                            TT(c["c0"][:, 1:2], c["c0"][:, 0:1], sink, ALU.max, [("cm0", c["par"]), "vecs"], [("cm1", c["par"])])

                        def st5(c):
                            TS(c["c0"][:, 2:3], c["c0"][:, 1:2], -1.0, ALU.mult, [("cm1", c["par"])], [("cm2", c["par"])])

                        def st6(c):
                            par, nk, c0 = c["par"], c["nk"], c["c0"]
                            ACT(p_f[par][:, 0:nk], s_sb[par][:, 0:nk], AF.Exp, [("s", par), ("cm2", par)], [("pf", par), ("cm3", par)],
                                bias=c0[:, 2:3], accum_out=c0[:, 3:4])

                        def st7(c):
                            ACT(c["c0"][:, 4:5], sink, AF.Exp, [("cm2", c["par"]), "vecs"], [("cm4", c["par"])], bias=c["c0"][:, 2:3])

                        def st8(c):
                            TT(c["c0"][:, 5:6], c["c0"][:, 3:4], c["c0"][:, 4:5], ALU.add, [("cm3", c["par"]), ("cm4", c["par"])], [("cm5", c["par"])])

                        def st9(c):
                            RECIP(c["c0"][:, 6:7], c["c0"][:, 5:6], [("cm5", c["par"])], [("cm6", c["par"])])

                        def st10(c):
                            par, nk = c["par"], c["nk"]
                            TS(p_b[par][:, 0:nk], p_f[par][:, 0:nk], c["c0"][:, 6:7], ALU.mult, [("pf", par), ("cm6", par)], [("pb", par)])

                        def st11(c):
                            par, nk = c["par"], c["nk"]
                            c["b2"] = nb()
                            pTp = psb[c["b2"]].bitcast(BF16)
                            c["pTp"] = pTp
                            for kb in range(nk // 128):
                                S.add("pe", lambda e, kb=kb, par=par, pTp=pTp: e.transpose(out=pTp[:, kb * 128:(kb + 1) * 128], in_=p_b[par][:, kb * 128:(kb + 1) * 128], identity=ident),
                                      reads=[("pb", par), "ident"], writes=[("ps", c["b2"])])

                        def st12(c):
                            COPY(pT_sb[c["par"]][:, 0:c["nk"]], c["pTp"][:, 0:c["nk"]], [("ps", c["b2"])], [("pT", c["par"])], eng="act")

                        def st13(c):
                            par, nk, lo = c["par"], c["nk"], c["lo"]
                            c["b3"] = nb()
                            nkb = nk // 128
                            for kb in range(nkb):
                                MM(psb[c["b3"]][:, 0:128], vE[:, kvh, lo // 128 + kb, :], pT_sb[par][:, kb * 128:(kb + 1) * 128], kb == 0, kb == nkb - 1,
                                   [("vE", lo // 128 + kb), ("pT", par)], [("ps", c["b3"])])

                        def st14(c):
                            COPY(yT[:, 8 + head, c["blk"] * 128:(c["blk"] + 1) * 128], psb[c["b3"]][:, 0:128], [("ps", c["b3"])], [("y", 8 + head)], eng="act")

                        for stg in (st1, st2, st3, st4, st5, st6, st7, st8, st9, st10, st11, st12, st13, st14):
                            for c in ctx:
                                stg(c)

        o = SCR
        maskM = cb(o, 1025); o += 2112
        v_bf = cb(o, 1024).rearrange("p (t d) -> p t d", t=8); o += 2048
        qT_f = cf(o, 1024); o += 4096
        q_dec = [cb(o, 1024), cb(o + 2048, 1024)]; o += 4096
        k_dec = [cb(o, 1024), cb(o + 2048, 1024)]; o += 4096
        prevb = [cb(o, 2048).rearrange("p (v n) -> p v n", n=16), cb(o + 4096, 2048).rearrange("p (v n) -> p v n", n=16)]; o += 8192
        o_pool = o
        tA, tB, tC, tD = (cf(o + i * 4096, 1024) for i in range(4)); o += 16384
        KTT = cb(o, 1024); o += 2048
        KT = cb(o, 1024).rearrange("p (t k) -> p t k", t=8); o += 2048
        dec = cf(o, 16); o += 64
        Sin_f = cf(o, 128); o += 512
        Sin_b = cb(o, 128); o += 256
        assert o <= OFF_B + 88064, o
        kvs = cf(o_pool, 2048)
        decm = cf(o_pool + 8192, 2048)
        kvs3 = kvs.rearrange("p (v n) -> p v n", n=16)
        decm3 = decm.rearrange("p (v n) -> p v n", n=16)
        Sall_f = cf(SCR + 2112 + 2048 + 4096, 2048)
        Sall3_f = Sall_f.rearrange("p (v n) -> p v n", n=16)
        S_out = qT_f.rearrange("p (h v) -> p h v", h=8)
        o_f, sgT = tA, tB
        scT = [KTT.rearrange("p (t k) -> p t k", t=8), KT]
        lbc = cf(OFF_M + 9216 - 192, 48)

        def rev(a, n):
            return bass.AP(tensor=a.tensor, offset=a.offset + n - 1, ap=[list(a.ap[0]), [-1, n]])

        def bc_inner(a, n_outer, n_inner):
            return bass.AP(tensor=a.tensor, offset=a.offset, ap=[list(a.ap[0]), [1, n_outer], [0, n_inner]])

        def bc_mid(a, n_mid, n_in, reverse=False):
            if reverse:
                return bass.AP(tensor=a.tensor, offset=a.offset + n_in - 1, ap=[list(a.ap[0]), [0, n_mid], [-1, n_in]])
            return bass.AP(tensor=a.tensor, offset=a.offset, ap=[list(a.ap[0]), [0, n_mid], [1, n_in]])

        def lb_all():
            d = lbc[:, 0:16]
            TT(d, vecs[:, V_A1:V_A1 + 16], vecs[:, V_A0:V_A0 + 16], ALU.subtract, ["vecs"], ["lbc"])
            ACT(d, d, AF.Exp, ["lbc"], ["lbc"])
            TS(d, d, 1.0, ALU.add, ["lbc"], ["lbc"])
            RECIP(d, d, ["lbc"], ["lbc"])
            TS(lbc[:, 16:32], d, -1.0, ALU.mult, ["lbc"], ["lbc"], s2=1.0, op1=ALU.add)
            TS(lbc[:, 32:48], lbc[:, 16:32], -1.0, ALU.mult, ["lbc"], ["lbc"])
            DMA(maskM, maskm_d.ap(), [], ["maskM"], "maskm", eng="pool")

        def hgrn_head(h, mode):
            dirs = [0] if mode == "state" else [0, 1]
            vcopy = lambda tt, b: COPY(v_bf[:, tt, :], psb[b][:, 0:128], [("ps", b)], ["vbf"], eng="act")
            if mode == "state":
                s = wload(win_s_d.ap()[h], 4096)
                w = wslot[s].rearrange("p (a k n) -> p a k n", a=2, k=16)
                tokmaj(w[:, 0:1], 128, vcopy)
                zsrc = {0: w[:, 1]}
            else:
                s = wload(win_a_d.ap()[h], 4096)
                wa = wslot[s].rearrange("p (a k n) -> p a k n", a=2, k=16)
                s = wload(win_b_d.ap()[h], 4096)
                wb = wslot[s].rearrange("p (a k n) -> p a k n", a=2, k=16)
                tokmaj(wb[:, 0:1], 128, vcopy)
                featmaj(wb[:, 1], lambda half, b: COPY(qT_f[:, half * 512:(half + 1) * 512], psb[b][:, :], [("ps", b)], ["qTf"], eng="act"))
                zsrc = {0: wa[:, 0], 1: wa[:, 1]}
            for dr in dirs:
                lbcol = lbc[:, dr * 8 + h:dr * 8 + h + 1]
                omlcol = lbc[:, 16 + dr * 8 + h:16 + dr * 8 + h + 1]
                nomlcol = lbc[:, 32 + dr * 8 + h:32 + dr * 8 + h + 1]
                featmaj(zsrc[dr], lambda half, b: ACT(tA[:, half * 512:(half + 1) * 512], psb[b][:, :], AF.Exp, [("ps", b)], ["A"], scale=-1.0))
                ACT(tB, tA, AF.Ln, ["A"], ["B"], bias=1.0)
                ACT(tA, tB, AF.Exp, ["B"], ["A"], scale=-1.0)
                ACT(tB, tA, AF.Ln, ["A", "lbc"], ["B"], scale=omlcol, bias=lbcol)
                TS(tC, tA, nomlcol, ALU.mult, ["A", "lbc"], ["C"], s2=omlcol, op1=ALU.add)
                if DBG.get("cut", 99) <= 1:
                    return
                if dr == 0:
                    S.add("dve", lambda e: e.tensor_tensor_scan(out=tD, data0=maskM[:, 0:1024], data1=tB, initial=0.0, op0=ALU.mult, op1=ALU.add),
                          reads=["B", "maskM"], writes=["D"])
                    tot = bass.AP(tensor=tD.tensor, offset=tD.offset + 63, ap=[list(tD.ap[0]), [64, 16]])
                else:
                    S.add("dve", lambda e: e.tensor_tensor_scan(out=rev(tD, 1024), data0=rev(maskM[:, 1:1025], 1024), data1=rev(tB, 1024), initial=0.0, op0=ALU.mult, op1=ALU.add),
                          reads=["B", "maskM"], writes=["D"])
                    tot = bass.AP(tensor=tD.tensor, offset=tD.offset, ap=[list(tD.ap[0]), [64, 16]])
                ACT(dec, tot, AF.Exp, ["D"], ["dec"])
                if DBG.get("cut", 99) <= 2:
                    return
                if mode == "out":
                    ACT(tA, tD, AF.Exp, ["D"], ["A"])
                    TT(q_dec[dr], qT_f, tA, ALU.mult, ["A", "qTf"], [("qdec", dr)])
                ACT(tB, tD, AF.Exp, ["D"], ["B"], scale=-1.0)
                TT(tB, tC, tB, ALU.mult, ["B", "C"], ["B"])
                if mode == "out":
                    COPY(k_dec[dr], tB, ["B"], [("kdec", dr)])
                TT(KTT.rearrange("p (n j) -> p n j", j=64), tB.rearrange("p (n j) -> p n j", j=64), bc_inner(dec, 16, 64), ALU.mult,
                   ["B", "dec"], ["KTT"])
                bT = nb()
                pT = psb[bT].bitcast(BF16)
                for tt in range(8):
                    S.add("pe", lambda e, tt=tt, pT=pT: e.transpose(out=pT[:, tt * 128:(tt + 1) * 128], in_=KTT[:, tt * 128:(tt + 1) * 128], identity=ident),
                          reads=["KTT", "ident"], writes=[("ps", bT)])
                COPY(KT.rearrange("p t k -> p (t k)"), pT[:, 0:1024], [("ps", bT)], ["KT"])
                if DBG.get("cut", 99) <= 3:
                    return
                for g8 in range(2):
                    kbs = [nb(), nb()]
                    cnt = [0, 0]
                    slots = []
                    for q8 in range(8):
                        npr = g8 * 8 + q8
                        n = npr if dr == 0 else 15 - npr
                        tt, par = n // 2, n % 2
                        po = par * 64
                        kb = kbs[par]
                        c0 = cnt[par] * 128
                        cnt[par] += 1
                        MM(psb[kb][:, c0:c0 + 128], KT[po:po + 64, tt, :], v_bf[po:po + 64, tt, :], True, True,
                           ["KT", "vbf"], [("ps", kb)])
                        slots.append((npr, kb, c0))
                    for par in range(2):
                        nprs = [npr for (npr, kb, c0) in slots if kb == kbs[par]]
                        a0 = nprs[0]
                        assert nprs == [a0 + 2 * i for i in range(4)], nprs
                        COPY(kvs3[:, :, a0:a0 + 7:2].rearrange("p v n -> p n v"), psb[kbs[par]][:, :].rearrange("p (n v) -> p n v", n=4),
                             [("ps", kbs[par])], ["A", "B"], eng=("act" if par else "dve"))
                if DBG.get("cut", 99) <= 4:
                    return
                COPY(decm3, bc_mid(dec, 128, 16, reverse=(dr == 1)), ["dec"], ["C", "D"])
                S.add("dve", lambda e: e.memset(decm3[:, :, 0:1], 0.0), reads=[], writes=["C", "D"])
                if DBG.get("cut", 99) <= 5:
                    return
                if dr == 1:
                    DMA(sin2, st_out.ap().rearrange("(r p) n -> p r n", p=128)[:, :, h * 128:(h + 1) * 128], ["st_out", "KTT"], ["KTT"], "stp")
                    TS(Sin_f, sin2[:, 0, :], gcol(V_M0, 0), ALU.mult, ["KTT", "vecs"], ["Sin"])
                    STT(Sin_f, sin2[:, 1, :], gcol(V_M1, 0), Sin_f, ALU.mult, ALU.add, ["KTT", "Sin", "vecs"], ["Sin"])
                    COPY(Sin_b, Sin_f, ["Sin"], ["Sinb"])
                    STT(kvs3[:, :, 0], Sin_f, dec[:, 15:16], kvs3[:, :, 0], ALU.mult, ALU.add, ["Sin", "dec", "A", "B"], ["A", "B"])
                if mode == "state":
                    S.add("dve", lambda e: e.tensor_tensor_scan(out=Sall_f, data0=decm, data1=kvs, initial=0.0, op0=ALU.mult, op1=ALU.add),
                          reads=["A", "B", "C", "D"], writes=["Sall"])
                    COPY(S_out[:, h, :], Sall3_f[:, :, 15], ["Sall"], ["Sout"])
                    return
                pv = prevb[dr]
                S.add("dve", lambda e, pv=pv: e.tensor_tensor_scan(out=pv.rearrange("p v n -> p (v n)"), data0=decm, data1=kvs, initial=0.0, op0=ALU.mult, op1=ALU.add),
                      reads=["A", "B", "C", "D"], writes=[("prev", dr)])
            for dr in dirs:
                Mi_f = cm[:, dr * 128:(dr + 1) * 128]
                for hb in range(2):
                    b = nb()
                    for t4 in range(4):
                        tt = hb * 4 + t4
                        MM(psb[b][:, t4 * 128:(t4 + 1) * 128], k_dec[dr][:, tt * 128:(tt + 1) * 128], q_dec[dr][:, tt * 128:(tt + 1) * 128], True, True,
                           [("kdec", dr), ("qdec", dr)], [("ps", b)])
                    TT(scT[dr][:, hb * 4:hb * 4 + 4, :], psb[b][:, :].rearrange("p (t k) -> p t k", t=4), bc_mid(Mi_f, 4, 128), ALU.mult,
                       [("ps", b), "cm"], ["KTT" if dr == 0 else "KT"])
            for hb in range(2):
                b = nb()
                mms = []
                for dr in dirs:
                    for t4 in range(4):
                        tt = hb * 4 + t4
                        cs0 = t4 * 128
                        mms.append((psb[b][:, cs0:cs0 + 128], v_bf[:, tt, :], scT[dr][:, tt, :], ["vbf", "KTT" if dr == 0 else "KT"]))
                        for n in (2 * tt, 2 * tt + 1):
                            npr = n if dr == 0 else 15 - n
                            c1 = cs0 + (n % 2) * 64
                            if npr == 0:
                                if dr == 0:
                                    continue
                                lt = Sin_b
                                rk = ["Sinb"]
                            else:
                                lt = prevb[dr][:, :, npr - 1]
                                rk = [("prev", dr)]
                            mms.append((psb[b][:, c1:c1 + 64], lt, q_dec[dr][:, n * 64:(n + 1) * 64], rk + [("qdec", dr)]))
                for idx, (o_, l_, r_, k_) in enumerate(mms):
                    MM(o_, l_, r_, idx == 0, idx == len(mms) - 1, k_, [("ps", b)])
                COPY(o_f[:, hb * 512:(hb + 1) * 512], psb[b][:, :], [("ps", b)], ["A"])
            s = wload(win_g_d.ap()[h], 2048)
            wg = wslot[s][:, 0:2048].rearrange("p (k n) -> p k n", k=16)
            featmaj(wg, lambda half, b: ACT(sgT[:, half * 512:(half + 1) * 512], psb[b][:, :], AF.Silu, [("ps", b)], ["B"]))
            for half in range(2):
                hs = slice(half * 512, (half + 1) * 512)
                b = nb()
                sqb = cb(o_pool + 8192, 512)
                ACT(sqb, o_f[:, hs], AF.Square, ["A"], ["C"])
                MM(psb[b][:, :], ones_b, sqb, True, True, ["C", "cmb"], [("ps", b)])
                ACT(tD[:, 0:512], psb[b][:, :], AF.Sqrt, [("ps", b)], ["D"], scale=1.0 / 128.0, bias=EPS)
                RECIP(tD[:, 0:512], tD[:, 0:512], ["D"], ["D"])
                STT(tD[:, 512:1024], o_f[:, hs], gcol(V_OG, h), tD[:, 0:512], ALU.mult, ALU.mult, ["A", "D", "vecs"], ["D2"])
                TT(yT[:, h, hs], tD[:, 512:1024], sgT[:, hs], ALU.mult, ["D2", "B"], [("y", h)])

        sin2 = cf(o_pool + 16384, 256).rearrange("p (r v) -> p r v", r=2)

        def hgrn_state_pass():
            lb_all()
            for h in range(DBG.get("ns", 8)):
                hgrn_head(h, "state")
            DMA(st_in.ap(), S_out.rearrange("p h v -> p (h v)"), ["Sout"], ["st_in"], "sti")
            S.add("pool", lambda e: e.collective_compute("AllGather", ALU.bypass, replica_groups=RG, ins=[st_in.ap().opt()], outs=[st_out.ap().opt()]),
                  reads=["st_in"], writes=["st_out"], dma_sem="cc2", dinc=1)

        def hgrn_out_pass():
            DMA(maskM, maskm_d.ap(), [], ["maskM"], "maskm", eng="pool")
            for h in range(DBG.get("no", 8)):
                hgrn_head(h, "out")

        mxh = [cf(OFF_H, 16 * 512).rearrange("p (c t) -> p c t", c=16), cf(OFF_B + 32768, 16 * 512).rearrange("p (c t) -> p c t", c=16)]

        def ffT2(c, half):
            return mxh[half][:, c, :]

        def mix_out():
            for i2 in range(8):
                s = wload(wout_d.ap()[i2], 4096)
                w = wslot[s].rearrange("p (i f n) -> p i f n", i=2, f=16)
                for ii in range(2):
                    i = 2 * i2 + ii
                    for half in range(2):
                        b = nb()
                        for fc in range(16):
                            MM(psb[b][:, :], w[:, ii, fc, :], yT[:, fc, half * 512:(half + 1) * 512], fc == 0, fc == 15,
                               [("w", s), ("y", fc)], [("ps", b)])
                        COPY(ffT2(i, half), psb[b][:, :], [("ps", b)], [("ff", i)])

        def mixer():
            prenorm(V_PREM)
            S.barrier()
            mixer_kv()
            S.barrier()
            hgrn_state_pass()
            S.barrier()
            attention()
            S.barrier()
            hgrn_out_pass()
            S.barrier()
            mix_out()
            S.barrier()
            postnorm_residual(V_POSTM, 1.0, src=ffT2)

        ffn(wgu_d[0], wd_d[0], V_PRE1, V_POST1)
        S.barrier()
        if stage >= 2:
            mixer()
            S.barrier()
        if stage >= 3:
            ffn(wgu_d[1], wd_d[1], V_PRE2, V_POST2)
            S.barrier()
        fin = []
        for q in range(4):
            fin.append(DMA(out_d.ap()[:, 4 * q:4 * q + 4, :], xT[:, 4 * q:4 * q + 4, :],
                           [("x", c) for c in range(4 * q, 4 * q + 4)], [], f"o{q}"))
        S.emit(final_waits=fin)
    return nc


def _t5_buckets(rel):
    nb = 16
    max_exact = 8
    bucket = (rel > 0).astype(np.int32) * nb
    n = np.abs(rel)
    large = max_exact + (np.log(np.maximum(n, 1) / max_exact) / np.log(128 / max_exact) * (nb - max_exact)).astype(np.int32)
    large = np.minimum(large, nb - 1)
    return bucket + np.where(n < max_exact, n, large).astype(np.int32)


def _host_consts():
    idx = np.arange(128)
    same = (idx[:, None] // 64) == (idx[None, :] // 64)
    s_, c_ = idx[:, None], idx[None, :]
    M_F = (same & (s_ <= c_)).astype(np.float32)
    M_B = (same & (s_ >= c_)).astype(np.float32)
    Mx_F = (same & (s_ < c_)).astype(np.float32)
    Mx_B = (same & (s_ > c_)).astype(np.float32)
    ones = np.ones((128, 128), np.float32)
    ind = np.stack([(idx < 64), (idx >= 64)], 1).astype(np.float32)
    cm = np.concatenate([M_F, M_B, Mx_F, Mx_B, ones, ind], 1)
    return np.ascontiguousarray(cm), np.eye(128, dtype=np.float32)


def _prep_shared(inp):
    sh = {}
    for n, (gu, dn) in enumerate([("w_ffn1_gate_up", "w_ffn1_down"), ("w_ffn2_gate_up", "w_ffn2_down")], 1):
        W = np.asarray(inp[gu])[0]
        Wk = W.reshape(16, 128, 2, NJ, 128)
        sh[f"wgu{n}"] = np.ascontiguousarray(Wk.transpose(3, 1, 2, 0, 4)).reshape(NJ, 128, 2 * 16 * 128)
        Wd = np.asarray(inp[dn])[0]
        Wdk = Wd.reshape(NG, JG, 128, 8, 2, 128)
        sh[f"wd{n}"] = np.ascontiguousarray(Wdk.transpose(0, 3, 2, 4, 1, 5)).reshape(NG, 8, 128, 2 * JG * 128)
    Wo = np.asarray(inp["w_mix_out"])[0]
    Wok = Wo.reshape(16, 128, 8, 2, 128)
    sh["wout"] = np.ascontiguousarray(Wok.transpose(2, 1, 3, 0, 4)).reshape(8, 128, 2 * 16 * 128)
    cm, ident = _host_consts()
    sh["cm"] = cm
    sh["ident"] = ident
    mm_ = np.ones((128, 1025), np.float32)
    mm_[:, ::64] = 0.0
    sh["maskm"] = mm_
    return sh


def _unit(Win, col0):
    return Win[:, col0:col0 + 128].reshape(16, 128, 128).transpose(1, 0, 2)


def _prep_core(inp, sh, c):
    b, r = c // 2, c % 2
    x = np.asarray(inp["x"])[b]
    xs = x[:T] if r == 0 else x[T:][::-1]
    m = dict(sh)
    m["xT"] = np.ascontiguousarray(xs.T.reshape(16, 128, T).transpose(1, 0, 2))
    Win = np.asarray(inp["w_mix_in"])[0]
    cF, cB = (2048, 3072) if r == 0 else (3072, 2048)
    def pair(u0, u1):
        return np.ascontiguousarray(np.stack([u0, u1], 1)).reshape(128, 2 * 16 * 128)
    m["win_s"] = np.stack([pair(_unit(Win, 1024 + h * 128), _unit(Win, cF + h * 128)) for h in range(8)])
    m["win_a"] = np.stack([pair(_unit(Win, cF + h * 128), _unit(Win, cB + h * 128)) for h in range(8)])
    m["win_b"] = np.stack([pair(_unit(Win, 1024 + h * 128), _unit(Win, h * 128)) for h in range(8)])
    m["win_g"] = np.stack([np.ascontiguousarray(_unit(Win, 4096 + h * 128)).reshape(128, 16 * 128) for h in range(8)])
    m["win_kv"] = np.stack([pair(_unit(Win, 6144), _unit(Win, 6144 + 128)), pair(_unit(Win, 6400), _unit(Win, 6400 + 128))])
    m["win_q"] = np.stack([pair(_unit(Win, 5120 + 2 * a * 128), _unit(Win, 5120 + (2 * a + 1) * 128)) for a in range(4)])
    vecs = np.zeros((128, NVEC), np.float32)
    for base, name in [(V_PRE1, "pre_norm_ffn1"), (V_POST1, "post_norm_ffn1"), (V_PREM, "pre_norm_mix"),
                       (V_POSTM, "post_norm_mix"), (V_PRE2, "pre_norm_ffn2"), (V_POST2, "post_norm_ffn2")]:
        vecs[:, base:base + 16] = np.asarray(inp[name])[0].reshape(16, 128).T
    vecs[:, V_OG:V_OG + 8] = np.asarray(inp["hgrn_out_norm"])[0].reshape(8, 128).T
    vecs[:, V_SINK:V_SINK + 8] = np.asarray(inp["attn_sink"])[0][None, :]
    vecs[:, V_M0] = 1.0 if r == 1 else 0.0
    vecs[:, V_M1] = 1.0 if r == 0 else 0.0
    lf = np.asarray(inp["hgrn_lower_bounds_fwd"])
    lb_ = np.asarray(inp["hgrn_lower_bounds_bwd"])
    aF, aB = (lf, lb_) if r == 0 else (lb_, lf)
    for a, base in ((0, V_A0), (1, V_A1)):
        vecs[:, base:base + 8] = aF[a].reshape(8, 128).T
        vecs[:, base + 8:base + 16] = aB[a].reshape(8, 128).T
    m["vecs"] = vecs
    araw = np.zeros((8, 128, 2, 2, 128), np.float32)
    for h in range(8):
        for a in range(2):
            araw[h, :, a, 0, :] = aF[a, h * 128:(h + 1) * 128][None, :]
            araw[h, :, a, 1, :] = aB[a, h * 128:(h + 1) * 128][None, :]
    m["araw"] = araw.reshape(8, 128, 512)
    cc = np.arange(128)[:, None]
    ss = np.arange(384)[None, :]
    rel = ss - 128 - cc
    relg = rel if r == 0 else -rel
    tab = np.asarray(inp["rel_bias_table"])
    bb = tab[_t5_buckets(relg)]
    valid = np.abs(rel) <= 128
    bias = np.empty((8, 128, 768), np.float32)
    for h in range(8):
        std = np.where(valid, bb[:, :, h], np.float32(NEG)).astype(np.float32)
        bias[h, :, :384] = std
        b7 = std.copy()
        b7[:, 256:384] = std[:, 256:384][:, ::-1]
        bias[h, :, 384:] = b7
    m["bias"] = bias
    return m


def kernel(**inputs):
    sh = _prep_shared(inputs)
    in_maps = [_prep_core(inputs, sh, c) for c in range(8)]
    nc = build()
    res = run_bass_kernel_spmd(nc, in_maps, core_ids=list(range(8)))
    out = np.empty((4, 2 * T, D), np.float32)
    for c in range(8):
        b, r = c // 2, c % 2
        oT = np.asarray(res.results[c]["outT"])
        o = oT.transpose(2, 1, 0).reshape(T, D)
        if r == 0:
            out[b, :T] = o
        else:
            out[b, T:] = o[::-1]
    return out
```

```python
import contextlib
import math
import numpy as np
import concourse.bass as bass
import concourse.mybir as mybir
from concourse.bass_utils import run_bass_kernel_spmd

F32 = mybir.dt.float32
BF16 = mybir.dt.bfloat16
AF = mybir.ActivationFunctionType
ALU = mybir.AluOpType
AX = mybir.AxisListType

D = 2048
T = 1024
DFF = 5632
NJ = 44
NG = 4
JG = 11
EPS = 1e-6
ENGS = ("pe", "act", "dve", "pool", "sp")
EPOCH = 20000
NEG = -10000.0
DBG = {}


class Op:
    __slots__ = ("eng", "fn", "deps", "is_dma", "dsem", "dval", "dinc", "idx", "signal", "tick")


class Sched:
    def __init__(self, nc):
        self.nc = nc
        self.ops = {e: [] for e in ENGS}
        self.last_writer = {}
        self.readers = {}
        self.dma_sem_count = {}
        self.all_ops = []
        self.barrier_deps = []
        self.since_barrier_dma = []

    def add(self, eng, fn, reads=(), writes=(), dma_sem=None, dinc=16):
        op = Op()
        op.eng = eng
        op.fn = fn
        op.is_dma = dma_sem is not None
        op.dsem = dma_sem
        op.dinc = dinc
        op.signal = False
        op.tick = None
        deps = list(self.barrier_deps)
        for k in reads:
            w = self.last_writer.get(k)
            if w is not None:
                deps.append(w)
        for k in writes:
            w = self.last_writer.get(k)
            if w is not None:
                deps.append(w)
            deps.extend(self.readers.get(k, ()))
        fd = []
        seen = set()
        for d in deps:
            if id(d) in seen:
                continue
            seen.add(id(d))
            fd.append(d)
        op.deps = fd
        if op.is_dma:
            c = self.dma_sem_count.get(dma_sem, 0) + dinc
            self.dma_sem_count[dma_sem] = c
            op.dval = c
            self.since_barrier_dma.append(op)
        for k in reads:
            self.readers.setdefault(k, []).append(op)
        for k in writes:
            self.last_writer[k] = op
            self.readers[k] = []
        self.ops[eng].append(op)
        self.all_ops.append(op)
        return op

    def barrier(self):
        deps = []
        for e in ENGS:
            for op in reversed(self.ops[e]):
                if not op.is_dma:
                    deps.append(op)
                    break
        deps.extend(self.since_barrier_dma)
        self.since_barrier_dma = []
        self.barrier_deps = deps

    def emit(self, final_waits=()):
        nc = self.nc
        for op in self.all_ops:
            nd = []
            for d in op.deps:
                if d.is_dma:
                    nd.append(d)
                    continue
                if d.eng == op.eng and (not op.is_dma) and op.eng == "pe":
                    continue
                nd.append(d)
            op.deps = nd
            for d in nd:
                if not d.is_dma:
                    d.signal = True
        for op in final_waits:
            if not op.is_dma:
                op.signal = True
        nticks = {}
        for e in ENGS:
            t = 0
            for op in self.ops[e]:
                if op.signal and not op.is_dma:
                    t += 1
                    op.tick = t
            nticks[e] = t
        stack = contextlib.ExitStack()
        esems = {}
        for e in ENGS:
            n = max(1, (nticks[e] + EPOCH - 1) // EPOCH)
            esems[e] = [stack.enter_context(nc.semaphore(f"c_{e}_{i}")) for i in range(n)]
        dsems = {k: stack.enter_context(nc.semaphore(f"d_{k}")) for k in self.dma_sem_count}

        def sig_of(d):
            if d.is_dma:
                return (dsems[d.dsem], d.dval, ("d", d.dsem))
            ep = (d.tick - 1) // EPOCH
            return (esems[d.eng][ep], (d.tick - 1) % EPOCH + 1, ("c", d.eng, ep))

        engobj = {"pe": "tensor", "act": "scalar", "dve": "vector", "pool": "gpsimd", "sp": "sync"}
        with stack:
            with nc.Block() as block:
                for e in ENGS:
                    ops = self.ops[e]
                    extra = list(final_waits) if e == "sp" else []
                    if not ops and not extra:
                        continue

                    def body(eng, ops=ops, e=e, extra=extra):
                        known = {}
                        for op in ops:
                            need = {}
                            for d in op.deps:
                                s, v, key = sig_of(d)
                                if known.get(key, 0) >= v:
                                    continue
                                if key not in need or need[key][1] < v:
                                    need[key] = (s, v)
                            for key, (s, v) in need.items():
                                eng.wait_ge(s, v)
                                known[key] = v
                            ins = op.fn(eng)
                            if op.is_dma:
                                ins.then_inc(dsems[op.dsem], op.dinc)
                            elif op.signal:
                                ep = (op.tick - 1) // EPOCH
                                ins.then_inc(esems[e][ep], 1)
                        for w in extra:
                            s, v, key = sig_of(w)
                            if known.get(key, 0) < v:
                                eng.wait_ge(s, v)
                                known[key] = v

                    getattr(block, engobj[e])(body)


V_PRE1, V_POST1, V_PREM, V_POSTM, V_PRE2, V_POST2 = 0, 16, 32, 48, 64, 80
V_OG, V_SINK, V_M0, V_M1 = 96, 104, 112, 113
NVEC = 160
V_A0, V_A1 = 128, 144

ARENA_BYTES = 211968
OFF_X, OFF_H, OFF_W, OFF_B, OFF_M = 0, 65536, 98304, 114688, 202752


def build(stage=99, debug=False):
    nc = bass.Bass("TRN2", target_bir_lowering=False)

    def din(name, shape, dt=F32):
        return nc.dram_tensor(name, list(shape), dt, kind="ExternalInput")

    xT_d = din("xT", [128, 16, T])
    wgu_d = [din("wgu1", [NJ, 128, 2 * 16 * 128]), din("wgu2", [NJ, 128, 2 * 16 * 128])]
    wd_d = [din("wd1", [NG, 8, 128, 2 * JG * 128]), din("wd2", [NG, 8, 128, 2 * JG * 128])]
    win_s_d = din("win_s", [8, 128, 2 * 16 * 128])
    win_a_d = din("win_a", [8, 128, 2 * 16 * 128])
    win_b_d = din("win_b", [8, 128, 2 * 16 * 128])
    win_g_d = din("win_g", [8, 128, 16 * 128])
    win_kv_d = din("win_kv", [2, 128, 2 * 16 * 128])
    win_q_d = din("win_q", [4, 128, 2 * 16 * 128])
    wout_d = din("wout", [8, 128, 2 * 16 * 128])
    vecs_d = din("vecs", [128, NVEC])
    araw_d = din("araw", [8, 128, 2 * 256])
    bias_d = din("bias", [8, 128, 768])
    cm_d = din("cm", [128, 4 * 128 + 128 + 2])
    ident_d = din("ident", [128, 128])
    maskm_d = din("maskm", [128, 1025])
    out_d = nc.dram_tensor("outT", [128, 16, T], F32, kind="ExternalOutput")
    dbg_d = nc.dram_tensor("dbg", [128, 16, T], F32, kind="ExternalOutput") if debug else None
    halo_in = nc.dram_tensor("halo_in", [128, 512], BF16)
    halo_out = nc.dram_tensor("halo_out", [256, 512], BF16)
    st_in = nc.dram_tensor("st_in", [8, 128, 128], F32)
    st_out = nc.dram_tensor("st_out", [8, 256, 128], F32)
    RG = [[0, 1], [2, 3], [4, 5], [6, 7]]

    S = Sched(nc)
    st = contextlib.ExitStack()
    with st:
        arena = st.enter_context(nc.sbuf_tensor("arena", [128, ARENA_BYTES // 4], F32))
        arena_bf = arena.bitcast(BF16)
        psb = [st.enter_context(nc.psum_tensor(f"ps{i}", [128, 512], F32)) for i in range(8)]

        def cf(off, n):
            assert off % 4 == 0
            return arena[:, off // 4: off // 4 + n]

        def cb(off, n):
            assert off % 2 == 0
            return arena_bf[:, off // 2: off // 2 + n]

        xT = cf(OFF_X, 16 * T).rearrange("p (c t) -> p c t", c=16)
        hT = cb(OFF_H, 16 * T).rearrange("p (c t) -> p c t", c=16)
        wslot = [cb(OFF_W, 4096), cb(OFF_W + 8192, 4096)]
        vecs = cf(OFF_M, NVEC)
        o_ = OFF_M + NVEC * 4
        cm = cf(o_, 4 * 128 + 128 + 2)
        o_ += (4 * 128 + 128 + 2) * 4
        o_ = (o_ + 31) // 32 * 32
        ident = cb(o_, 128)
        o_ += 256
        ftmp = [cf(o_, 512), cf(o_ + 2048, 512)]
        o_ += 4096
        ones_b = cb(o_, 128)
        o_ += 256
        mk_b = cb(o_, 512)
        o_ += 1024
        indb = cb(o_, 2)
        o_ += 32
        assert o_ <= ARENA_BYTES, o_
        M_F, M_B, Mx_F, Mx_B = (cm[:, i * 128:(i + 1) * 128] for i in range(4))
        ones_f = cm[:, 512:640]
        ind = cm[:, 640:642]

        ffT = cf(OFF_B, 16 * T).rearrange("p (c t) -> p c t", c=16)
        aT = cb(OFF_B + 65536, JG * T).rearrange("p (j t) -> p j t", j=JG)
        nsq = [cb(OFF_B + 65536, 512), cb(OFF_B + 65536 + 2048, 512)]
        nrs = cf(OFF_B + 65536 + 4096, 512)
        nrstd = cf(OFF_B + 65536 + 6144, 512)
        ntmp = [cf(OFF_B + 65536 + 8192, 512), cf(OFF_B + 65536 + 10240, 512)]

        state = {"pb": 0, "ws": 0, "u": 0}

        def nb():
            b = state["pb"]
            state["pb"] = (b + 1) % 8
            return b

        def uid():
            state["u"] += 1
            return state["u"]

        def wload(src_ap, nelem):
            s = state["ws"]
            state["ws"] = 1 - s
            dst = wslot[s][:, 0:nelem]
            S.add("pool", lambda e: e.dma_start(out=dst, in_=src_ap), writes=[("w", s)], dma_sem=f"w{s}")
            return s

        def MM(out, lhsT, rhs, start, stop, reads, writes):
            return S.add("pe", lambda e: e.matmul(out, lhsT=lhsT, rhs=rhs, start=start, stop=stop),
                         reads=reads, writes=writes)

        def ACT(out, in_, func, reads, writes, scale=1.0, bias=0.0, accum_out=None):
            def f(e):
                kw = {}
                if accum_out is not None:
                    kw["accum_out"] = accum_out
                return e.activation(out=out, in_=in_, func=func, scale=scale, bias=bias, **kw)
            return S.add("act", f, reads=reads, writes=writes)

        def TS(out, in0, s1, op0, reads, writes, s2=None, op1=None, eng="dve"):
            def f(e):
                if op1 is None:
                    return e.tensor_scalar(out=out, in0=in0, scalar1=s1, scalar2=None, op0=op0)
                return e.tensor_scalar(out=out, in0=in0, scalar1=s1, scalar2=s2, op0=op0, op1=op1)
            return S.add(eng, f, reads=reads, writes=writes)

        def TT(out, in0, in1, op, reads, writes, eng="dve"):
            return S.add(eng, lambda e: e.tensor_tensor(out=out, in0=in0, in1=in1, op=op), reads=reads, writes=writes)

        def STT(out, in0, scalar, in1, op0, op1, reads, writes):
            return S.add("dve", lambda e: e.scalar_tensor_tensor(out=out, in0=in0, scalar=scalar, in1=in1, op0=op0, op1=op1),
                         reads=reads, writes=writes)

        def RECIP(out, in_, reads, writes):
            return S.add("dve", lambda e: e.reciprocal(out=out, in_=in_), reads=reads, writes=writes)

        def COPY(out, in_, reads, writes, eng="dve"):
            if eng == "act":
                return S.add("act", lambda e: e.activation(out=out, in_=in_, func=AF.Copy), reads=reads, writes=writes)
            return S.add(eng, lambda e: e.tensor_copy(out=out, in_=in_), reads=reads, writes=writes)

        def DMA(out, in_, reads, writes, sem, eng="sp"):
            return S.add(eng, lambda e: e.dma_start(out=out, in_=in_), reads=reads, writes=writes, dma_sem=sem)

        DMA(vecs, vecs_d.ap(), [], ["vecs"], "c0")
        DMA(cm, cm_d.ap(), [], ["cm"], "c1")
        DMA(ident, ident_d.ap(), [], ["ident"], "c2", eng="pool")
        DMA(ones_b, cm_d.ap()[:, 512:640], [], ["cmb"], "c3", eng="pool")
        DMA(mk_b, cm_d.ap()[:, 0:512], [], ["cmb"], "c4", eng="pool")
        DMA(indb, cm_d.ap()[:, 640:642], [], ["cmb"], "c5", eng="pool")
        for q in range(4):
            DMA(xT[:, 4 * q:4 * q + 4, :], xT_d.ap()[:, 4 * q:4 * q + 4, :], [], [("x", c) for c in range(4 * q, 4 * q + 4)], f"x{q}")

        def gcol(base, c):
            return vecs[:, base + c: base + c + 1]

        def rstd_of(src, srckey, nfeat_chunks, half, scratch_sq, rs, rstd, denom):
            b = nb()
            for c in range(nfeat_chunks):
                sq = scratch_sq[c % 2]
                ACT(sq, src(c, half), AF.Square, [srckey(c)], [("nsq", c % 2)])
                MM(psb[b][:, :], ones_b, sq, c == 0, c == nfeat_chunks - 1, [("nsq", c % 2), "cmb"], [("ps", b)])
            ACT(rs, psb[b][:, :], AF.Sqrt, [("ps", b)], ["nrs"], scale=1.0 / denom, bias=EPS)
            RECIP(rstd, rs, ["nrs"], ["nrstd"])

        def prenorm(gbase):
            for half in range(2):
                hs = slice(half * 512, (half + 1) * 512)
                rstd_of(lambda c, h: xT[:, c, h * 512:(h + 1) * 512], lambda c: ("x", c), 16, half, nsq, nrs, nrstd, float(D))
                for c in range(16):
                    STT(hT[:, c, hs], xT[:, c, hs], gcol(gbase, c), nrstd, ALU.mult, ALU.mult,
                        [("x", c), "nrstd", "vecs"], [("h", c, half)])

        def postnorm_residual(gbase, coef, src=None):
            if src is None:
                src = lambda c, h: ffT[:, c, h * 512:(h + 1) * 512]
            for half in range(2):
                hs = slice(half * 512, (half + 1) * 512)
                rstd_of(src, lambda c: ("ff", c), 16, half, nsq, nrs, nrstd, float(D))
                for c in range(16):
                    tmp = ntmp[c % 2]
                    STT(tmp, src(c, half), gcol(gbase, c), nrstd, ALU.mult, ALU.mult,
                        [("ff", c), "nrstd", "vecs"], [("ntmp", c % 2)])
                    STT(xT[:, c, hs], tmp, coef, xT[:, c, hs], ALU.mult, ALU.add,
                        [("ntmp", c % 2), ("x", c)], [("x", c)])

        def ffn(wgu, wd, gpre, gpost):
            prenorm(gpre)
            for g in range(NG):
                for jj in range(JG):
                    j = g * JG + jj
                    s = wload(wgu.ap()[j], 4096)
                    w = wslot[s].rearrange("p (a k n) -> p a k n", a=2, k=16)
                    banks = [nb() for _ in range(4)]
                    for a in range(2):
                        for kc in range(16):
                            for half in range(2):
                                b = banks[a * 2 + half]
                                MM(psb[b][:, :], w[:, a, kc, :], hT[:, kc, half * 512:(half + 1) * 512], kc == 0, kc == 15,
                                   [("w", s), ("h", kc, half)], [("ps", b)])
                    for half in range(2):
                        bg, bu = banks[half], banks[2 + half]
                        t = ftmp[half]
                        ACT(t, psb[bg][:, :], AF.Silu, [("ps", bg)], [("ftmp", half)])
                        TT(aT[:, jj, half * 512:(half + 1) * 512], psb[bu][:, :], t, ALU.mult,
                           [("ps", bu), ("ftmp", half)], [("a", jj, half)])
                for i2 in range(8):
                    s = wload(wd.ap()[g, i2], 2 * JG * 128)
                    w = wslot[s][:, 0:2 * JG * 128].rearrange("p (i j n) -> p i j n", i=2, j=JG)
                    for ii in range(2):
                        i = 2 * i2 + ii
                        for half in range(2):
                            b = nb()
                            for jj in range(JG):
                                MM(psb[b][:, :], w[:, ii, jj, :], aT[:, jj, half * 512:(half + 1) * 512], jj == 0, jj == JG - 1,
                                   [("w", s), ("a", jj, half)], [("ps", b)])
                            dst = ffT[:, i, half * 512:(half + 1) * 512]
                            if g == 0:
                                COPY(dst, psb[b][:, :], [("ps", b)], [("ff", i)])
                            else:
                                TT(dst, psb[b][:, :], dst, ALU.add, [("ps", b), ("ff", i)], [("ff", i)])
            S.barrier()
            postnorm_residual(gpost, 0.5)


        yT = cb(OFF_B, 16 * T).rearrange("p (c t) -> p c t", c=16)
        kTe = cb(OFF_B + 32768, 2 * 1152).rearrange("p (u t) -> p u t", u=2)
        vE = cb(OFF_B + 37376, 2 * 9 * 128).rearrange("p (u t d) -> p u t d", u=2, t=9)
        SCR = OFF_B + 41984
        ISQ = 1.0 / math.sqrt(128.0)

        def featmaj(w_unit, dst_fn, evac="dve"):
            for half in range(2):
                b = nb()
                for kc in range(16):
                    MM(psb[b][:, :], w_unit[:, kc, :], hT[:, kc, half * 512:(half + 1) * 512], kc == 0, kc == 15,
                       [("w", 0), ("w", 1), ("h", kc, half)], [("ps", b)])
                dst_fn(half, b)

        def tokmaj(w_pair, ncols, fn):
            for tt in range(8):
                b = nb()
                for kc in range(16):
                    MM(psb[b][:, 0:ncols].rearrange("p (u n) -> p u n", n=128), hT[:, kc, tt * 128:(tt + 1) * 128], w_pair[:, :, kc, :], kc == 0, kc == 15,
                       [("w", 0), ("w", 1), ("h", kc, tt // 4)], [("ps", b)])
                fn(tt, b)

        def mixer_kv():
            s = wload(win_kv_d.ap()[0], 4096)
            w = wslot[s].rearrange("p (a k n) -> p a k n", a=2, k=16)
            for u in range(2):
                featmaj(w[:, u], lambda half, b, u=u: COPY(kTe[:, u, half * 512:(half + 1) * 512], psb[b][:, :], [("ps", b)], [("kT", u)]))
            s = wload(win_kv_d.ap()[1], 4096)
            w2 = wslot[s].rearrange("p (a k n) -> p a k n", a=2, k=16)
            tokmaj(w2, 256, lambda tt, b: COPY(vE[:, :, tt, :], psb[b][:, 0:256].rearrange("p (u d) -> p u d", u=2), [("ps", b)], [("vE", tt)]))
            hp = cb(SCR, 1024).rearrange("p (r n) -> p r n", r=2)
            d1 = [DMA(halo_in.ap()[:, u * 128:(u + 1) * 128], kTe[:, u, 896:1024], [("kT", u)], ["halo_in"], f"hi{u}") for u in range(2)]
            d1 += [DMA(halo_in.ap()[:, 256 + u * 128:256 + (u + 1) * 128], vE[:, u, 7, :], [("vE", 7)], ["halo_in"], f"hi{2 + u}") for u in range(2)]
            S.add("pool", lambda e: e.collective_compute("AllGather", ALU.bypass, replica_groups=RG, ins=[halo_in.ap().opt()], outs=[halo_out.ap().opt()]),
                  reads=["halo_in"], writes=["halo_out"], dma_sem="cc1", dinc=1)
            DMA(hp, halo_out.ap().rearrange("(r p) n -> p r n", p=128), ["halo_out"], ["hp"], "hp")
            tmpb = cf(SCR + 2048, 512)
            TS(tmpb, hp[:, 0, :], gcol(V_M0, 0), ALU.mult, ["hp", "vecs"], ["hpt"])
            res = cb(SCR + 4096, 512)
            STT(res, hp[:, 1, :], gcol(V_M1, 0), tmpb, ALU.mult, ALU.add, ["hp", "hpt", "vecs"], ["hres"])
            for u in range(2):
                COPY(kTe[:, u, 1024:1152], res[:, u * 128:(u + 1) * 128], ["hres"], [("kT", u)])
                COPY(vE[:, u, 8, :], res[:, 256 + u * 128:256 + (u + 1) * 128], ["hres"], [("vE", 8)])

        def attention():
            NP = 8
            qsb = cb(SCR, 1024)
            bias_sb = cf(SCR + 2048, 768)
            s_sb = [cf(SCR + 5120 + i * 1536, 384) for i in range(NP)]
            p_f = [cf(SCR + 17408 + i * 1536, 384) for i in range(NP)]
            p_b = [cb(SCR + 29696 + i * 768, 384) for i in range(NP)]
            pT_sb = [cb(SCR + 35840 + i * 768, 384) for i in range(NP)]
            cols = cf(SCR + 41984, 8 * NP)
            assert SCR + 41984 + 32 * NP <= OFF_B + 88064
            for a in range(4):
                s = wload(win_q_d.ap()[a], 4096)
                w = wslot[s].rearrange("p (a k n) -> p a k n", a=2, k=16)
                for u in range(2):
                    head = 2 * a + u
                    kvh = head // 4
                    featmaj(w[:, u], lambda half, b: COPY(qsb[:, half * 512:(half + 1) * 512], psb[b][:, :], [("ps", b)], ["qsb"]))
                    DMA(bias_sb, bias_d.ap()[head], [], ["bias"], "bias")
                    sink = gcol(V_SINK, head)
                    for grp in range(8 // NP):
                        ctx = []
                        for par in range(NP):
                            blk = grp * NP + par
                            lo = max(0, blk - 1) * 128
                            hi = (blk + 2) * 128
                            ctx.append(dict(blk=blk, par=par, lo=lo, nk=hi - lo, c0=cols[:, par * 8:par * 8 + 8],
                                            bsl=(bias_sb[:, 128:384] if blk == 0 else (bias_sb[:, 384:768] if blk == 7 else bias_sb[:, 0:384]))))

                        def st1(c):
                            c["b"] = nb()
                            MM(psb[c["b"]][:, 0:c["nk"]], qsb[:, c["blk"] * 128:(c["blk"] + 1) * 128], kTe[:, kvh, c["lo"]:c["lo"] + c["nk"]], True, True,
                               ["qsb", ("kT", kvh)], [("ps", c["b"])])

                        def st2(c):
                            STT(s_sb[c["par"]][:, 0:c["nk"]], psb[c["b"]][:, 0:c["nk"]], ISQ, c["bsl"], ALU.mult, ALU.add, [("ps", c["b"]), "bias"], [("s", c["par"])])

                        def st3(c):
                            par, nk, c0 = c["par"], c["nk"], c["c0"]
                            S.add("dve", lambda e: e.reduce_max(out=c0[:, 0:1], in_=s_sb[par][:, 0:nk], axis=AX.X),
                                  reads=[("s", par)], writes=[("cm0", par)])

                        def st4(c):
                            TT(c["c0"][:, 1:2], c["c0"][:, 0:1], sink, ALU.max, [("cm0", c["par"]), "vecs"], [("cm1", c["par"])])

                        def st5(c):
                            TS(c["c0"][:, 2:3], c["c0"][:, 1:2], -1.0, ALU.mult, [("cm1", c["par"])], [("cm2", c["par"])])

                        def st6(c):
                            par, nk, c0 = c["par"], c["nk"], c["c0"]
                            ACT(p_f[par][:, 0:nk], s_sb[par][:, 0:nk], AF.Exp, [("s", par), ("cm2", par)], [("pf", par), ("cm3", par)],
                                bias=c0[:, 2:3], accum_out=c0[:, 3:4])

                        def st7(c):
                            ACT(c["c0"][:, 4:5], sink, AF.Exp, [("cm2", c["par"]), "vecs"], [("cm4", c["par"])], bias=c["c0"][:, 2:3])

                        def st8(c):
                            TT(c["c0"][:, 5:6], c["c0"][:, 3:4], c["c0"][:, 4:5], ALU.add, [("cm3", c["par"]), ("cm4", c["par"])], [("cm5", c["par"])])

                        def st9(c):
                            RECIP(c["c0"][:, 6:7], c["c0"][:, 5:6], [("cm5", c["par"])], [("cm6", c["par"])])

                        def st10(c):
                            par, nk = c["par"], c["nk"]
                            TS(p_b[par][:, 0:nk], p_f[par][:, 0:nk], c["c0"][:, 6:7], ALU.mult, [("pf", par), ("cm6", par)], [("pb", par)])

                        def st11(c):
                            par, nk = c["par"], c["nk"]
                            c["b2"] = nb()
                            pTp = psb[c["b2"]].bitcast(BF16)
                            c["pTp"] = pTp
                            for kb in range(nk // 128):
                                S.add("pe", lambda e, kb=kb, par=par, pTp=pTp: e.transpose(out=pTp[:, kb * 128:(kb + 1) * 128], in_=p_b[par][:, kb * 128:(kb + 1) * 128], identity=ident),
                                      reads=[("pb", par), "ident"], writes=[("ps", c["b2"])])

                        def st12(c):
                            COPY(pT_sb[c["par"]][:, 0:c["nk"]], c["pTp"][:, 0:c["nk"]], [("ps", c["b2"])], [("pT", c["par"])], eng="act")

                        def st13(c):
                            par, nk, lo = c["par"], c["nk"], c["lo"]
                            c["b3"] = nb()
                            nkb = nk // 128
                            for kb in range(nkb):
                                MM(psb[c["b3"]][:, 0:128], vE[:, kvh, lo // 128 + kb, :], pT_sb[par][:, kb * 128:(kb + 1) * 128], kb == 0, kb == nkb - 1,
                                   [("vE", lo // 128 + kb), ("pT", par)], [("ps", c["b3"])])

                        def st14(c):
                            COPY(yT[:, 8 + head, c["blk"] * 128:(c["blk"] + 1) * 128], psb[c["b3"]][:, 0:128], [("ps", c["b3"])], [("y", 8 + head)], eng="act")

                        for stg in (st1, st2, st3, st4, st5, st6, st7, st8, st9, st10, st11, st12, st13, st14):
                            for c in ctx:
                                stg(c)

        o = SCR
        maskM = cb(o, 1025); o += 2112
        v_bf = cb(o, 1024).rearrange("p (t d) -> p t d", t=8); o += 2048
        qT_f = cf(o, 1024); o += 4096
        q_dec = [cb(o, 1024), cb(o + 8192, 1024)]
        k_dec = [cb(o + 2048, 1024), cb(o + 8192 + 2048, 1024)]
        prevb = [cb(o + 4096, 2048).rearrange("p (v n) -> p v n", n=16), cb(o + 8192 + 4096, 2048).rearrange("p (v n) -> p v n", n=16)]
        SallF = cf(o + 8192, 2048)
        SallF3 = SallF.rearrange("p (v n) -> p v n", n=16)
        o += 16384
        o_pool = o
        tA, tB, tC, tD = (cf(o + i * 4096, 1024) for i in range(4)); o += 16384
        KTT = cb(o, 1024); o += 2048
        KT = cb(o, 1024).rearrange("p (t k) -> p t k", t=8); o += 2048
        dec = cf(o, 16); o += 64
        Sin_f = cf(o, 128); o += 512
        Sin_b = cb(o, 128); o += 256
        assert o <= OFF_B + 88064, o
        kvs = cf(o_pool, 2048)
        decm = cf(o_pool + 8192, 2048)
        kvs3 = kvs.rearrange("p (v n) -> p v n", n=16)
        decm3 = decm.rearrange("p (v n) -> p v n", n=16)
        Sall_f = cf(SCR + 2112 + 2048 + 4096, 2048)
        Sall3_f = Sall_f.rearrange("p (v n) -> p v n", n=16)
        S_out = qT_f.rearrange("p (h v) -> p h v", h=8)
        o_f, sgT = tA, tB
        scT = [KTT.rearrange("p (t k) -> p t k", t=8), KT]
        lbc = cf(OFF_M + 9216 - 192, 48)

        def rev(a, n):
            return bass.AP(tensor=a.tensor, offset=a.offset + n - 1, ap=[list(a.ap[0]), [-1, n]])

        def bc_inner(a, n_outer, n_inner):
            return bass.AP(tensor=a.tensor, offset=a.offset, ap=[list(a.ap[0]), [1, n_outer], [0, n_inner]])

        def bc_mid(a, n_mid, n_in, reverse=False):
            if reverse:
                return bass.AP(tensor=a.tensor, offset=a.offset + n_in - 1, ap=[list(a.ap[0]), [0, n_mid], [-1, n_in]])
            return bass.AP(tensor=a.tensor, offset=a.offset, ap=[list(a.ap[0]), [0, n_mid], [1, n_in]])

        def lb_all():
            d = lbc[:, 0:16]
            TT(d, vecs[:, V_A1:V_A1 + 16], vecs[:, V_A0:V_A0 + 16], ALU.subtract, ["vecs"], ["lbc"])
            ACT(d, d, AF.Exp, ["lbc"], ["lbc"])
            TS(d, d, 1.0, ALU.add, ["lbc"], ["lbc"])
            RECIP(d, d, ["lbc"], ["lbc"])
            TS(lbc[:, 16:32], d, -1.0, ALU.mult, ["lbc"], ["lbc"], s2=1.0, op1=ALU.add)
            TS(lbc[:, 32:48], lbc[:, 16:32], -1.0, ALU.mult, ["lbc"], ["lbc"])
            DMA(maskM, maskm_d.ap(), [], ["maskM"], "maskm", eng="pool")

        def hgrn_head(h, mode):
            dirs = [0] if mode == "state" else [0, 1]
            vcopy = lambda tt, b: COPY(v_bf[:, tt, :], psb[b][:, 0:128], [("ps", b)], ["vbf"], eng="act")
            if mode == "state":
                s = wload(win_s_d.ap()[h], 4096)
                w = wslot[s].rearrange("p (a k n) -> p a k n", a=2, k=16)
                tokmaj(w[:, 0:1], 128, vcopy)
                zsrc = {0: w[:, 1]}
            else:
                s = wload(win_a_d.ap()[h], 4096)
                wa = wslot[s].rearrange("p (a k n) -> p a k n", a=2, k=16)
                s = wload(win_b_d.ap()[h], 4096)
                wb = wslot[s].rearrange("p (a k n) -> p a k n", a=2, k=16)
                tokmaj(wb[:, 0:1], 128, vcopy)
                featmaj(wb[:, 1], lambda half, b: COPY(qT_f[:, half * 512:(half + 1) * 512], psb[b][:, :], [("ps", b)], ["qTf"], eng="act"))
                zsrc = {0: wa[:, 0], 1: wa[:, 1]}
            for dr in dirs:
                lbcol = lbc[:, dr * 8 + h:dr * 8 + h + 1]
                omlcol = lbc[:, 16 + dr * 8 + h:16 + dr * 8 + h + 1]
                nomlcol = lbc[:, 32 + dr * 8 + h:32 + dr * 8 + h + 1]
                featmaj(zsrc[dr], lambda half, b: ACT(tA[:, half * 512:(half + 1) * 512], psb[b][:, :], AF.Exp, [("ps", b)], ["A"], scale=-1.0))
                ACT(tB, tA, AF.Ln, ["A"], ["B"], bias=1.0)
                ACT(tA, tB, AF.Exp, ["B"], ["A"], scale=-1.0)
                ACT(tB, tA, AF.Ln, ["A", "lbc"], ["B"], scale=omlcol, bias=lbcol)
                TS(tC, tA, nomlcol, ALU.mult, ["A", "lbc"], ["C"], s2=omlcol, op1=ALU.add)
                if DBG.get("cut", 99) <= 1:
                    return
                if dr == 0:
                    S.add("dve", lambda e: e.tensor_tensor_scan(out=tD, data0=maskM[:, 0:1024], data1=tB, initial=0.0, op0=ALU.mult, op1=ALU.add),
                          reads=["B", "maskM"], writes=["D"])
                    tot = bass.AP(tensor=tD.tensor, offset=tD.offset + 63, ap=[list(tD.ap[0]), [64, 16]])
                else:
                    S.add("dve", lambda e: e.tensor_tensor_scan(out=rev(tD, 1024), data0=rev(maskM[:, 1:1025], 1024), data1=rev(tB, 1024), initial=0.0, op0=ALU.mult, op1=ALU.add),
                          reads=["B", "maskM"], writes=["D"])
                    tot = bass.AP(tensor=tD.tensor, offset=tD.offset, ap=[list(tD.ap[0]), [64, 16]])
                ACT(dec, tot, AF.Exp, ["D"], ["dec"])
                if DBG.get("cut", 99) <= 2:
                    return
                if mode == "out":
                    ACT(tA, tD, AF.Exp, ["D"], ["A"])
                    TT(q_dec[dr], qT_f, tA, ALU.mult, ["A", "qTf"], [("qdec", dr)])
                ACT(tB, tD, AF.Exp, ["D"], ["B"], scale=-1.0)
                TT(tB, tC, tB, ALU.mult, ["B", "C"], ["B"])
                if mode == "out":
                    COPY(k_dec[dr], tB, ["B"], [("kdec", dr)])
                TT(KTT.rearrange("p (n j) -> p n j", j=64), tB.rearrange("p (n j) -> p n j", j=64), bc_inner(dec, 16, 64), ALU.mult,
                   ["B", "dec"], ["KTT"])
                bT = nb()
                pT = psb[bT].bitcast(BF16)
                for tt in range(8):
                    S.add("pe", lambda e, tt=tt, pT=pT: e.transpose(out=pT[:, tt * 128:(tt + 1) * 128], in_=KTT[:, tt * 128:(tt + 1) * 128], identity=ident),
                          reads=["KTT", "ident"], writes=[("ps", bT)])
                COPY(KT.rearrange("p t k -> p (t k)"), pT[:, 0:1024], [("ps", bT)], ["KT"])
                if DBG.get("cut", 99) <= 3:
                    return
                for g8 in range(2):
                    kbs = [nb(), nb()]
                    cnt = [0, 0]
                    slots = []
                    for q8 in range(8):
                        npr = g8 * 8 + q8
                        n = npr if dr == 0 else 15 - npr
                        tt, par = n // 2, n % 2
                        po = par * 64
                        kb = kbs[par]
                        c0 = cnt[par] * 128
                        cnt[par] += 1
                        MM(psb[kb][:, c0:c0 + 128], KT[po:po + 64, tt, :], v_bf[po:po + 64, tt, :], True, True,
                           ["KT", "vbf"], [("ps", kb)])
                        slots.append((npr, kb, c0))
                    for par in range(2):
                        nprs = [npr for (npr, kb, c0) in slots if kb == kbs[par]]
                        a0 = nprs[0]
                        assert nprs == [a0 + 2 * i for i in range(4)], nprs
                        COPY(kvs3[:, :, a0:a0 + 7:2].rearrange("p v n -> p n v"), psb[kbs[par]][:, :].rearrange("p (n v) -> p n v", n=4),
                             [("ps", kbs[par])], ["A", "B"], eng=("act" if par else "dve"))
                if DBG.get("cut", 99) <= 4:
                    return
                COPY(decm3, bc_mid(dec, 128, 16, reverse=(dr == 1)), ["dec"], ["C", "D"])
                S.add("dve", lambda e: e.memset(decm3[:, :, 0:1], 0.0), reads=[], writes=["C", "D"])
                if DBG.get("cut", 99) <= 5:
                    return
                if dr == 1:
                    DMA(sin2, st_out.ap()[h].rearrange("(r p) n -> p r n", p=128), [("st_out", h), "KTT"], ["KTT"], "stp")
                    TS(Sin_f, sin2[:, 0, :], gcol(V_M0, 0), ALU.mult, ["KTT", "vecs"], ["Sin"])
                    STT(Sin_f, sin2[:, 1, :], gcol(V_M1, 0), Sin_f, ALU.mult, ALU.add, ["KTT", "Sin", "vecs"], ["Sin"])
                    COPY(Sin_b, Sin_f, ["Sin"], ["Sinb"])
                    STT(kvs3[:, :, 0], Sin_f, dec[:, 15:16], kvs3[:, :, 0], ALU.mult, ALU.add, ["Sin", "dec", "A", "B"], ["A", "B"])
                if mode == "state":
                    S.add("dve", lambda e: e.tensor_tensor_scan(out=Sall_f, data0=decm, data1=kvs, initial=0.0, op0=ALU.mult, op1=ALU.add),
                          reads=["A", "B", "C", "D"], writes=["Sall"])
                    COPY(S_out[:, h, :], Sall3_f[:, :, 15], ["Sall"], ["Sout"])
                    return
                pv = prevb[dr]
                if dr == 0:
                    XK = [("qdec", 1), ("kdec", 1), ("prev", 1)]
                    S.add("dve", lambda e: e.tensor_tensor_scan(out=SallF, data0=decm, data1=kvs, initial=0.0, op0=ALU.mult, op1=ALU.add),
                          reads=["A", "B", "C", "D"], writes=XK)
                    COPY(Sin_f, SallF3[:, :, 15], XK, ["Sin"])
                    COPY(pv.rearrange("p v n -> p (v n)"), SallF, XK, [("prev", 0)], eng="act")
                    DMA(st_in.ap()[h], Sin_f, ["Sin"], [("st_in", h)], "sti")
                    S.add("pool", lambda e, h=h: e.collective_compute("AllGather", ALU.bypass, replica_groups=RG, ins=[st_in.ap()[h].opt()], outs=[st_out.ap()[h].opt()]),
                          reads=[("st_in", h)], writes=[("st_out", h)], dma_sem="cc2", dinc=1)
                else:
                    S.add("dve", lambda e, pv=pv: e.tensor_tensor_scan(out=pv.rearrange("p v n -> p (v n)"), data0=decm, data1=kvs, initial=0.0, op0=ALU.mult, op1=ALU.add),
                          reads=["A", "B", "C", "D"], writes=[("prev", dr)])
            for dr in dirs:
                Mi_f = cm[:, dr * 128:(dr + 1) * 128]
                for hb in range(2):
                    b = nb()
                    for t4 in range(4):
                        tt = hb * 4 + t4
                        MM(psb[b][:, t4 * 128:(t4 + 1) * 128], k_dec[dr][:, tt * 128:(tt + 1) * 128], q_dec[dr][:, tt * 128:(tt + 1) * 128], True, True,
                           [("kdec", dr), ("qdec", dr)], [("ps", b)])
                    TT(scT[dr][:, hb * 4:hb * 4 + 4, :], psb[b][:, :].rearrange("p (t k) -> p t k", t=4), bc_mid(Mi_f, 4, 128), ALU.mult,
                       [("ps", b), "cm"], ["KTT" if dr == 0 else "KT"])
            for hb in range(2):
                b = nb()
                mms = []
                for dr in dirs:
                    for t4 in range(4):
                        tt = hb * 4 + t4
                        cs0 = t4 * 128
                        mms.append((psb[b][:, cs0:cs0 + 128], v_bf[:, tt, :], scT[dr][:, tt, :], ["vbf", "KTT" if dr == 0 else "KT"]))
                        for n in (2 * tt, 2 * tt + 1):
                            npr = n if dr == 0 else 15 - n
                            c1 = cs0 + (n % 2) * 64
                            if npr == 0:
                                if dr == 0:
                                    continue
                                lt = Sin_b
                                rk = ["Sinb"]
                            else:
                                lt = prevb[dr][:, :, npr - 1]
                                rk = [("prev", dr)]
                            mms.append((psb[b][:, c1:c1 + 64], lt, q_dec[dr][:, n * 64:(n + 1) * 64], rk + [("qdec", dr)]))
                for idx, (o_, l_, r_, k_) in enumerate(mms):
                    MM(o_, l_, r_, idx == 0, idx == len(mms) - 1, k_, [("ps", b)])
                COPY(o_f[:, hb * 512:(hb + 1) * 512], psb[b][:, :], [("ps", b)], ["A"])
            s = wload(win_g_d.ap()[h], 2048)
            wg = wslot[s][:, 0:2048].rearrange("p (k n) -> p k n", k=16)
            featmaj(wg, lambda half, b: ACT(sgT[:, half * 512:(half + 1) * 512], psb[b][:, :], AF.Silu, [("ps", b)], ["B"]))
            for half in range(2):
                hs = slice(half * 512, (half + 1) * 512)
                b = nb()
                sqb = cb(o_pool + 8192, 512)
                ACT(sqb, o_f[:, hs], AF.Square, ["A"], ["C"])
                MM(psb[b][:, :], ones_b, sqb, True, True, ["C", "cmb"], [("ps", b)])
                ACT(tD[:, 0:512], psb[b][:, :], AF.Sqrt, [("ps", b)], ["D"], scale=1.0 / 128.0, bias=EPS)
                RECIP(tD[:, 0:512], tD[:, 0:512], ["D"], ["D"])
                STT(tD[:, 512:1024], o_f[:, hs], gcol(V_OG, h), tD[:, 0:512], ALU.mult, ALU.mult, ["A", "D", "vecs"], ["D2"])
                TT(yT[:, h, hs], tD[:, 512:1024], sgT[:, hs], ALU.mult, ["D2", "B"], [("y", h)])

        sin2 = cf(o_pool + 16384, 256).rearrange("p (r v) -> p r v", r=2)

        def hgrn_state_pass():
            lb_all()
            for h in range(DBG.get("ns", 8)):
                hgrn_head(h, "state")
            DMA(st_in.ap(), S_out.rearrange("p h v -> p (h v)"), ["Sout"], ["st_in"], "sti")
            S.add("pool", lambda e: e.collective_compute("AllGather", ALU.bypass, replica_groups=RG, ins=[st_in.ap().opt()], outs=[st_out.ap().opt()]),
                  reads=["st_in"], writes=["st_out"], dma_sem="cc2", dinc=1)

        def hgrn_out_pass():
            lb_all()
            for h in range(DBG.get("no", 8)):
                hgrn_head(h, "out")

        mxh = [cf(OFF_H, 16 * 512).rearrange("p (c t) -> p c t", c=16), cf(OFF_B + 32768, 16 * 512).rearrange("p (c t) -> p c t", c=16)]

        def ffT2(c, half):
            return mxh[half][:, c, :]

        def mix_out():
            for i2 in range(8):
                s = wload(wout_d.ap()[i2], 4096)
                w = wslot[s].rearrange("p (i f n) -> p i f n", i=2, f=16)
                for ii in range(2):
                    i = 2 * i2 + ii
                    for half in range(2):
                        b = nb()
                        for fc in range(16):
                            MM(psb[b][:, :], w[:, ii, fc, :], yT[:, fc, half * 512:(half + 1) * 512], fc == 0, fc == 15,
                               [("w", s), ("y", fc)], [("ps", b)])
                        COPY(ffT2(i, half), psb[b][:, :], [("ps", b)], [("ff", i)])

        def mixer():
            prenorm(V_PREM)
            S.barrier()
            mixer_kv()
            S.barrier()
            attention()
            S.barrier()
            hgrn_out_pass()
            S.barrier()
            mix_out()
            S.barrier()
            postnorm_residual(V_POSTM, 1.0, src=ffT2)

        ffn(wgu_d[0], wd_d[0], V_PRE1, V_POST1)
        S.barrier()
        if stage >= 2:
            mixer()
            S.barrier()
        if stage >= 3:
            ffn(wgu_d[1], wd_d[1], V_PRE2, V_POST2)
            S.barrier()
        fin = []
        for q in range(4):
            fin.append(DMA(out_d.ap()[:, 4 * q:4 * q + 4, :], xT[:, 4 * q:4 * q + 4, :],
                           [("x", c) for c in range(4 * q, 4 * q + 4)], [], f"o{q}"))
        S.emit(final_waits=fin)
    return nc


def _t5_buckets(rel):
    nb = 16
    max_exact = 8
    bucket = (rel > 0).astype(np.int32) * nb
    n = np.abs(rel)
    large = max_exact + (np.log(np.maximum(n, 1) / max_exact) / np.log(128 / max_exact) * (nb - max_exact)).astype(np.int32)
    large = np.minimum(large, nb - 1)
    return bucket + np.where(n < max_exact, n, large).astype(np.int32)


def _host_consts():
    idx = np.arange(128)
    same = (idx[:, None] // 64) == (idx[None, :] // 64)
    s_, c_ = idx[:, None], idx[None, :]
    M_F = (same & (s_ <= c_)).astype(np.float32)
    M_B = (same & (s_ >= c_)).astype(np.float32)
    Mx_F = (same & (s_ < c_)).astype(np.float32)
    Mx_B = (same & (s_ > c_)).astype(np.float32)
    ones = np.ones((128, 128), np.float32)
    ind = np.stack([(idx < 64), (idx >= 64)], 1).astype(np.float32)
    cm = np.concatenate([M_F, M_B, Mx_F, Mx_B, ones, ind], 1)
    return np.ascontiguousarray(cm), np.eye(128, dtype=np.float32)


def _prep_shared(inp):
    sh = {}
    for n, (gu, dn) in enumerate([("w_ffn1_gate_up", "w_ffn1_down"), ("w_ffn2_gate_up", "w_ffn2_down")], 1):
        W = np.asarray(inp[gu])[0]
        Wk = W.reshape(16, 128, 2, NJ, 128)
        sh[f"wgu{n}"] = np.ascontiguousarray(Wk.transpose(3, 1, 2, 0, 4)).reshape(NJ, 128, 2 * 16 * 128)
        Wd = np.asarray(inp[dn])[0]
        Wdk = Wd.reshape(NG, JG, 128, 8, 2, 128)
        sh[f"wd{n}"] = np.ascontiguousarray(Wdk.transpose(0, 3, 2, 4, 1, 5)).reshape(NG, 8, 128, 2 * JG * 128)
    Wo = np.asarray(inp["w_mix_out"])[0]
    Wok = Wo.reshape(16, 128, 8, 2, 128)
    sh["wout"] = np.ascontiguousarray(Wok.transpose(2, 1, 3, 0, 4)).reshape(8, 128, 2 * 16 * 128)
    cm, ident = _host_consts()
    sh["cm"] = cm
    sh["ident"] = ident
    mm_ = np.ones((128, 1025), np.float32)
    mm_[:, ::64] = 0.0
    sh["maskm"] = mm_
    return sh


def _unit(Win, col0):
    return Win[:, col0:col0 + 128].reshape(16, 128, 128).transpose(1, 0, 2)


def _prep_core(inp, sh, c):
    b, r = c // 2, c % 2
    x = np.asarray(inp["x"])[b]
    xs = x[:T] if r == 0 else x[T:][::-1]
    m = dict(sh)
    m["xT"] = np.ascontiguousarray(xs.T.reshape(16, 128, T).transpose(1, 0, 2))
    Win = np.asarray(inp["w_mix_in"])[0]
    cF, cB = (2048, 3072) if r == 0 else (3072, 2048)
    def pair(u0, u1):
        return np.ascontiguousarray(np.stack([u0, u1], 1)).reshape(128, 2 * 16 * 128)
    m["win_s"] = np.stack([pair(_unit(Win, 1024 + h * 128), _unit(Win, cF + h * 128)) for h in range(8)])
    m["win_a"] = np.stack([pair(_unit(Win, cF + h * 128), _unit(Win, cB + h * 128)) for h in range(8)])
    m["win_b"] = np.stack([pair(_unit(Win, 1024 + h * 128), _unit(Win, h * 128)) for h in range(8)])
    m["win_g"] = np.stack([np.ascontiguousarray(_unit(Win, 4096 + h * 128)).reshape(128, 16 * 128) for h in range(8)])
    m["win_kv"] = np.stack([pair(_unit(Win, 6144), _unit(Win, 6144 + 128)), pair(_unit(Win, 6400), _unit(Win, 6400 + 128))])
    m["win_q"] = np.stack([pair(_unit(Win, 5120 + 2 * a * 128), _unit(Win, 5120 + (2 * a + 1) * 128)) for a in range(4)])
    vecs = np.zeros((128, NVEC), np.float32)
    for base, name in [(V_PRE1, "pre_norm_ffn1"), (V_POST1, "post_norm_ffn1"), (V_PREM, "pre_norm_mix"),
                       (V_POSTM, "post_norm_mix"), (V_PRE2, "pre_norm_ffn2"), (V_POST2, "post_norm_ffn2")]:
        vecs[:, base:base + 16] = np.asarray(inp[name])[0].reshape(16, 128).T
    vecs[:, V_OG:V_OG + 8] = np.asarray(inp["hgrn_out_norm"])[0].reshape(8, 128).T
    vecs[:, V_SINK:V_SINK + 8] = np.asarray(inp["attn_sink"])[0][None, :]
    vecs[:, V_M0] = 1.0 if r == 1 else 0.0
    vecs[:, V_M1] = 1.0 if r == 0 else 0.0
    lf = np.asarray(inp["hgrn_lower_bounds_fwd"])
    lb_ = np.asarray(inp["hgrn_lower_bounds_bwd"])
    aF, aB = (lf, lb_) if r == 0 else (lb_, lf)
    for a, base in ((0, V_A0), (1, V_A1)):
        vecs[:, base:base + 8] = aF[a].reshape(8, 128).T
        vecs[:, base + 8:base + 16] = aB[a].reshape(8, 128).T
    m["vecs"] = vecs
    araw = np.zeros((8, 128, 2, 2, 128), np.float32)
    for h in range(8):
        for a in range(2):
            araw[h, :, a, 0, :] = aF[a, h * 128:(h + 1) * 128][None, :]
            araw[h, :, a, 1, :] = aB[a, h * 128:(h + 1) * 128][None, :]
    m["araw"] = araw.reshape(8, 128, 512)
    cc = np.arange(128)[:, None]
    ss = np.arange(384)[None, :]
    rel = ss - 128 - cc
    relg = rel if r == 0 else -rel
    tab = np.asarray(inp["rel_bias_table"])
    bb = tab[_t5_buckets(relg)]
    valid = np.abs(rel) <= 128
    bias = np.empty((8, 128, 768), np.float32)
    for h in range(8):
        std = np.where(valid, bb[:, :, h], np.float32(NEG)).astype(np.float32)
        bias[h, :, :384] = std
        b7 = std.copy()
        b7[:, 256:384] = std[:, 256:384][:, ::-1]
        bias[h, :, 384:] = b7
    m["bias"] = bias
    return m


def kernel(**inputs):
    sh = _prep_shared(inputs)
    in_maps = [_prep_core(inputs, sh, c) for c in range(8)]
    nc = build()
    res = run_bass_kernel_spmd(nc, in_maps, core_ids=list(range(8)))
    out = np.empty((4, 2 * T, D), np.float32)
    for c in range(8):
        b, r = c // 2, c % 2
        oT = np.asarray(res.results[c]["outT"])
        o = oT.transpose(2, 1, 0).reshape(T, D)
        if r == 0:
            out[b, :T] = o
        else:
            out[b, T:] = o[::-1]
    return out
```
